# Optimizing a Trainium2 kernel written in Bass

```python
import math
import jax, jax.numpy as jnp
from jax import lax
import numpy as np

D_MODEL = 1024
BATCH = 2
SEQ = 8192
DEPTH = 2
DEC_BATCH = 128
DEC_SEQ = 8
PAST_LEN = 2048
PAGE_SIZE = 128

D_RNN = D_MODEL
RNN_HEADS = 16
RNN_BLOCK = D_RNN // RNN_HEADS
CONV_WIDTH = 4
LRU_C = 8.0
D_SSM = D_MODEL // 2
SSM_GROUP = 16
SSM_GROUPS = D_SSM // SSM_GROUP
SSM_STATE = 64
ATTN_HEADS = 8
HEAD_DIM = 64
D_ATTN = ATTN_HEADS * HEAD_DIM
MOBA_BLOCK = 256
MOBA_TOPK = 3
Q_CHUNK = 128
ROPE_THETA = 10000.0
N_BRANCH = 3
D_FF = ((8 * D_MODEL // 3 + 255) // 256) * 256
LN_EPS = 1e-5
NEG_INF = -1e30
DN_ALPHA = (2.0 * DEPTH) ** 0.25
DN_BETA = (8.0 * DEPTH) ** -0.25
N_IN = D_RNN + D_SSM + 3 * D_ATTN + N_BRANCH * D_MODEL
SPLITS = (D_RNN, D_RNN + D_SSM, D_RNN + D_SSM + D_ATTN, D_RNN + D_SSM + 2 * D_ATTN, D_RNN + D_SSM + 3 * D_ATTN)

kernel_name = 'hybrid_rglru_s5_moba_step'


def layer_norm(x, g, b):
    xf = x.astype(jnp.float32)
    mu = jnp.mean(xf, axis=-1, keepdims=True)
    var = jnp.mean(jnp.square(xf - mu), axis=-1, keepdims=True)
    return ((xf - mu) * lax.rsqrt(var + LN_EPS) * g.astype(jnp.float32) + b.astype(jnp.float32))


def rope(x, pos):
    half = HEAD_DIM // 2
    inv = jnp.power(ROPE_THETA, -jnp.arange(half, dtype=jnp.float32) * (2.0 / HEAD_DIM))
    ang = pos.astype(jnp.float32)[:, None, None] * inv
    cos, sin = jnp.cos(ang), jnp.sin(ang)
    xf = x.astype(jnp.float32)
    x1, x2 = xf[..., :half], xf[..., half:]
    return jnp.concatenate([x1 * cos - x2 * sin, x2 * cos + x1 * sin], axis=-1).astype(x.dtype)


def causal_conv(x, buf, w, b):
    t = x.shape[1]
    xp = jnp.concatenate([buf, x], axis=1)
    y = b + w[0] * xp[:, 0:t]
    for j in range(1, CONV_WIDTH):
        y = y + w[j] * xp[:, j:j + t]
    return y, xp[:, t:]


def _lin_combine(e1, e2):
    a1, b1 = e1
    a2, b2 = e2
    return a1 * a2, a2 * b1 + b2


def _cplx_combine(e1, e2):
    ar1, ai1, br1, bi1 = e1
    ar2, ai2, br2, bi2 = e2
    return (ar1 * ar2 - ai1 * ai2, ar1 * ai2 + ai1 * ar2,
            ar2 * br1 - ai2 * bi1 + br2, ar2 * bi1 + ai2 * br1 + bi2)


def rg_lru(x, h0, w_a, b_a, w_x, b_x, lam):
    bsz, t, _ = x.shape
    f32 = jnp.float32
    xf = x.astype(f32)
    xh = xf.reshape(bsz, t, RNN_HEADS, RNN_BLOCK)
    r = jax.nn.sigmoid(jnp.einsum('bthi,hij->bthj', xh, w_a.astype(f32)).reshape(bsz, t, D_RNN) + b_a.astype(f32))
    i = jax.nn.sigmoid(jnp.einsum('bthi,hij->bthj', xh, w_x.astype(f32)).reshape(bsz, t, D_RNN) + b_x.astype(f32))
    log_a = -LRU_C * r * jax.nn.softplus(-lam.astype(f32))
    a = jnp.exp(log_a)
    mult = jnp.sqrt(jnp.maximum(-jnp.expm1(2.0 * log_a), 0.0))
    bt = mult * (i * xf)
    bt = bt.at[:, 0].add(a[:, 0] * h0.astype(f32))
    _, h = lax.associative_scan(_lin_combine, (a, bt), axis=1)
    return h.astype(x.dtype), h[:, -1].astype(x.dtype)


def s5_ssm(u, h0r, h0i, lam_re, lam_im, b_re, b_im, c_re, c_im, d, log_step, w_glu, b_glu):
    bsz, t, _ = u.shape
    f32 = jnp.float32
    dt = jnp.exp(log_step.astype(f32))[:, None]
    lr, li = lam_re.astype(f32), lam_im.astype(f32)
    mag = jnp.exp(lr * dt)
    abr, abi = mag * jnp.cos(li * dt), mag * jnp.sin(li * dt)
    nr, ni = abr - 1.0, abi
    den = lr * lr + li * li
    zr, zi = (nr * lr + ni * li) / den, (ni * lr - nr * li) / den
    br_, bi_ = b_re.astype(f32), b_im.astype(f32)
    bbr = zr[..., None] * br_ - zi[..., None] * bi_
    bbi = zr[..., None] * bi_ + zi[..., None] * br_
    ug = u.astype(f32).reshape(bsz, t, SSM_GROUPS, SSM_GROUP)
    bur = jnp.einsum('btgc,gpc->btgp', ug, bbr)
    bui = jnp.einsum('btgc,gpc->btgp', ug, bbi)
    h0r, h0i = h0r.astype(f32), h0i.astype(f32)
    bur = bur.at[:, 0].add(abr * h0r - abi * h0i)
    bui = bui.at[:, 0].add(abr * h0i + abi * h0r)
    ar = jnp.broadcast_to(abr, bur.shape)
    ai = jnp.broadcast_to(abi, bui.shape)
    _, _, hr, hi = lax.associative_scan(_cplx_combine, (ar, ai, bur, bui), axis=1)
    y = jnp.einsum('btgp,gcp->btgc', hr, c_re.astype(f32)) - jnp.einsum('btgp,gcp->btgc', hi, c_im.astype(f32))
    y = y.reshape(bsz, t, D_SSM) + d.astype(f32) * u.astype(f32)
    g = jax.nn.gelu(y)
    out = g * jax.nn.sigmoid(g @ w_glu.astype(f32) + b_glu.astype(f32))
    return out.astype(u.dtype), hr[:, -1].astype(u.dtype), hi[:, -1].astype(u.dtype)


def moba_head(q, k, v, pos0):
    f32 = jnp.float32
    bsz, tq, _ = q.shape
    length = k.shape[1]
    nb = -(-length // MOBA_BLOCK)
    pad = (nb + 1) * MOBA_BLOCK - length
    kp = jnp.pad(k, ((0, 0), (0, pad), (0, 0)))
    vp = jnp.pad(v, ((0, 0), (0, pad), (0, 0)))
    kblk = kp.reshape(bsz, nb + 1, MOBA_BLOCK, HEAD_DIM)
    vblk = vp.reshape(bsz, nb + 1, MOBA_BLOCK, HEAD_DIM)
    kmean = jnp.mean(kblk[:, :nb].astype(f32), axis=2)
    ksel = min(MOBA_TOPK, nb)
    qc = Q_CHUNK if tq % Q_CHUNK == 0 else tq
    nq = tq // qc
    qs = q.reshape(bsz, nq, qc, HEAD_DIM).transpose(1, 0, 2, 3)
    bidx = jnp.arange(bsz)[:, None, None]
    scale = HEAD_DIM ** -0.5

    def chunk(args):
        ci, qq = args
        qf = qq.astype(f32)
        start = pos0 + ci * qc
        p = start + jnp.arange(qc)
        qblk = p // MOBA_BLOCK
        gate = jnp.einsum('bqd,bnd->bqn', qf, kmean)
        past = jnp.arange(nb)[None, :] < qblk[:, None]
        gate = jnp.where(past[None], gate, -jnp.inf)
        _, sel = lax.top_k(gate, ksel)
        valid = sel < qblk[None, :, None]
        kg = kblk[bidx, sel]
        vg = vblk[bidx, sel]
        s_sel = jnp.einsum('bqd,bqjsd->bqjs', qf, kg.astype(f32)) * scale
        s_sel = jnp.where(valid[..., None], s_sel, NEG_INF).reshape(bsz, qc, ksel * MOBA_BLOCK)
        wstart = (start // MOBA_BLOCK) * MOBA_BLOCK
        kw = lax.dynamic_slice_in_dim(kp, wstart, 2 * MOBA_BLOCK, axis=1)
        vw = lax.dynamic_slice_in_dim(vp, wstart, 2 * MOBA_BLOCK, axis=1)
        kpos = wstart + jnp.arange(2 * MOBA_BLOCK)
        own = ((kpos[None, :] // MOBA_BLOCK) == qblk[:, None]) & (kpos[None, :] <= p[:, None])
        s_own = jnp.einsum('bqd,bkd->bqk', qf, kw.astype(f32)) * scale
        s_own = jnp.where(own[None], s_own, NEG_INF)
        prob = jax.nn.softmax(jnp.concatenate([s_sel, s_own], axis=-1), axis=-1)
        p_sel = prob[..., :ksel * MOBA_BLOCK].reshape(bsz, qc, ksel, MOBA_BLOCK)
        p_own = prob[..., ksel * MOBA_BLOCK:]
        o = jnp.einsum('bqjs,bqjsd->bqd', p_sel, vg.astype(f32)) + jnp.einsum('bqk,bkd->bqd', p_own, vw.astype(f32))
        return o.astype(q.dtype)

    out = lax.map(chunk, (jnp.arange(nq), qs))
    return out.transpose(1, 0, 2, 3).reshape(bsz, tq, HEAD_DIM)


def moba_attention(q, k, v, pos0):
    def one_head(args):
        qh, kh, vh = args
        return moba_head(qh, kh, vh, pos0)
    return lax.map(one_head, (q, k, v))


def trunk_layer(x, pos0, past_k, past_v, conv_buf, h0, s5_h0_re, s5_h0_im,
                w_in, conv_w, conv_b, rg_w_a, rg_b_a, rg_w_x, rg_b_x, rg_lambda,
                s5_lambda_re, s5_lambda_im, s5_b_re, s5_b_im, s5_c_re, s5_c_im, s5_d, s5_log_step, s5_w_glu, s5_b_glu,
                w_br_rnn, w_br_ssm, w_br_attn, w_out, ln1_g, ln1_b, w_ffn_in, w_ffn_out, ln2_g, ln2_b):
    bsz, t, _ = x.shape
    proj = x @ w_in
    xr, us, q, k, v, g = jnp.split(proj, SPLITS, axis=-1)
    xc, conv_new = causal_conv(xr, conv_buf, conv_w, conv_b)
    ya, h_new = rg_lru(xc, h0, rg_w_a, rg_b_a, rg_w_x, rg_b_x, rg_lambda)
    ys, s5r, s5i = s5_ssm(us, s5_h0_re, s5_h0_im, s5_lambda_re, s5_lambda_im, s5_b_re, s5_b_im,
                          s5_c_re, s5_c_im, s5_d, s5_log_step, s5_w_glu, s5_b_glu)
    pos = pos0 + jnp.arange(t)
    qr = rope(q.reshape(bsz, t, ATTN_HEADS, HEAD_DIM), pos)
    kr = rope(k.reshape(bsz, t, ATTN_HEADS, HEAD_DIM), pos)
    vr = v.reshape(bsz, t, ATTN_HEADS, HEAD_DIM)
    k_all = jnp.concatenate([past_k.astype(kr.dtype), kr], axis=1).transpose(2, 0, 1, 3)
    v_all = jnp.concatenate([past_v.astype(vr.dtype), vr], axis=1).transpose(2, 0, 1, 3)
    yc = moba_attention(qr.transpose(2, 0, 1, 3), k_all, v_all, pos0)
    yc = yc.transpose(1, 2, 0, 3).reshape(bsz, t, D_ATTN)
    gates = jax.nn.sigmoid(g.reshape(bsz, t, N_BRANCH, D_MODEL))
    merged = (gates[:, :, 0] * (ya @ w_br_rnn) + gates[:, :, 1] * (ys @ w_br_ssm)
              + gates[:, :, 2] * (yc @ w_br_attn))
    x = layer_norm(DN_ALPHA * x + merged @ w_out, ln1_g, ln1_b).astype(x.dtype)
    hg, hu = jnp.split(x @ w_ffn_in, 2, axis=-1)
    x = layer_norm(DN_ALPHA * x + (jax.nn.silu(hg) * hu) @ w_ffn_out, ln2_g, ln2_b).astype(x.dtype)
    return x, kr, vr, conv_new, h_new, s5r, s5i


def setup_inputs(seed: int = 0) -> dict:
    key = jax.random.key(seed)
    ks = iter(jax.random.split(key, 40))
    f32 = jnp.float32

    def nrm(shape, s=1.0):
        return jax.random.normal(next(ks), shape, f32) * s

    n_pages = PAST_LEN // PAGE_SIZE
    in_use = DEC_BATCH * n_pages
    n_pool = in_use + max(1, in_use // 4)
    out = {}
    out['x_prompt'] = nrm((BATCH, SEQ, D_MODEL))
    out['x_sample'] = nrm((DEC_BATCH, DEC_SEQ, D_MODEL))
    out['cache_k'] = nrm((DEPTH, n_pool, PAGE_SIZE, ATTN_HEADS, HEAD_DIM))
    out['cache_v'] = nrm((DEPTH, n_pool, PAGE_SIZE, ATTN_HEADS, HEAD_DIM))
    out['state_conv'] = nrm((DEPTH, DEC_BATCH, CONV_WIDTH - 1, D_RNN))
    out['state_rglru'] = nrm((DEPTH, DEC_BATCH, D_RNN), 0.5)
    out['state_s5_re'] = nrm((DEPTH, DEC_BATCH, SSM_GROUPS, SSM_STATE), 0.5)
    out['state_s5_im'] = nrm((DEPTH, DEC_BATCH, SSM_GROUPS, SSM_STATE), 0.5)
    out['page_table'] = jax.random.permutation(next(ks), n_pool)[:in_use].reshape(DEC_BATCH, n_pages).astype(jnp.int32)
    out['w_in'] = nrm((DEPTH, D_MODEL, N_IN), D_MODEL ** -0.5)
    out['conv_w'] = nrm((DEPTH, CONV_WIDTH, D_RNN), CONV_WIDTH ** -0.5)
    out['conv_b'] = nrm((DEPTH, D_RNN), 0.01)
    out['rg_w_a'] = nrm((DEPTH, RNN_HEADS, RNN_BLOCK, RNN_BLOCK), RNN_BLOCK ** -0.5)
    out['rg_b_a'] = nrm((DEPTH, D_RNN), 0.01)
    out['rg_w_x'] = nrm((DEPTH, RNN_HEADS, RNN_BLOCK, RNN_BLOCK), RNN_BLOCK ** -0.5)
    out['rg_b_x'] = nrm((DEPTH, D_RNN), 0.01)
    u = jax.random.uniform(next(ks), (DEPTH, D_RNN), f32, minval=0.9, maxval=0.999)
    s = jnp.power(u, 1.0 / LRU_C)
    out['rg_lambda'] = jnp.log(s) - jnp.log1p(-s)
    out['s5_lambda_re'] = -0.5 + nrm((DEPTH, SSM_GROUPS, SSM_STATE), 0.01)
    out['s5_lambda_im'] = np.pi * jnp.arange(SSM_STATE, dtype=f32) + nrm((DEPTH, SSM_GROUPS, SSM_STATE), 0.01)
    out['s5_b_re'] = nrm((DEPTH, SSM_GROUPS, SSM_STATE, SSM_GROUP), (2 * SSM_GROUP) ** -0.5)
    out['s5_b_im'] = nrm((DEPTH, SSM_GROUPS, SSM_STATE, SSM_GROUP), (2 * SSM_GROUP) ** -0.5)
    out['s5_c_re'] = nrm((DEPTH, SSM_GROUPS, SSM_GROUP, SSM_STATE), SSM_STATE ** -0.5)
    out['s5_c_im'] = nrm((DEPTH, SSM_GROUPS, SSM_GROUP, SSM_STATE), SSM_STATE ** -0.5)
    out['s5_d'] = nrm((DEPTH, D_SSM))
    out['s5_log_step'] = jax.random.uniform(next(ks), (DEPTH, SSM_GROUPS), f32, minval=math.log(1e-3), maxval=math.log(1e-1))
    out['s5_w_glu'] = nrm((DEPTH, D_SSM, D_SSM), D_SSM ** -0.5)
    out['s5_b_glu'] = nrm((DEPTH, D_SSM), 0.01)
    out['w_br_rnn'] = nrm((DEPTH, D_RNN, D_MODEL), D_RNN ** -0.5)
    out['w_br_ssm'] = nrm((DEPTH, D_SSM, D_MODEL), D_SSM ** -0.5)
    out['w_br_attn'] = nrm((DEPTH, D_ATTN, D_MODEL), D_ATTN ** -0.5)
    out['w_out'] = nrm((DEPTH, D_MODEL, D_MODEL), D_MODEL ** -0.5 * DN_BETA)
    out['ln1_g'] = 1.0 + nrm((DEPTH, D_MODEL), 0.01)
    out['ln1_b'] = nrm((DEPTH, D_MODEL), 0.01)
    out['w_ffn_in'] = nrm((DEPTH, D_MODEL, 2 * D_FF), D_MODEL ** -0.5)
    out['w_ffn_out'] = nrm((DEPTH, D_FF, D_MODEL), D_FF ** -0.5 * DN_BETA)
    out['ln2_g'] = 1.0 + nrm((DEPTH, D_MODEL), 0.01)
    out['ln2_b'] = nrm((DEPTH, D_MODEL), 0.01)
    return out


def reference(x_prompt, x_sample, cache_k, cache_v, state_conv, state_rglru, state_s5_re, state_s5_im, page_table,
              w_in, conv_w, conv_b, rg_w_a, rg_b_a, rg_w_x, rg_b_x, rg_lambda,
              s5_lambda_re, s5_lambda_im, s5_b_re, s5_b_im, s5_c_re, s5_c_im, s5_d, s5_log_step, s5_w_glu, s5_b_glu,
              w_br_rnn, w_br_ssm, w_br_attn, w_out, ln1_g, ln1_b, w_ffn_in, w_ffn_out, ln2_g, ln2_b):
    bp = x_prompt.shape[0]
    bs = x_sample.shape[0]
    n_pages = page_table.shape[1]
    dt = x_prompt.dtype
    yp, ys = x_prompt, x_sample
    kp_l, vp_l, cp_l, hp_l, srp_l, sip_l = [], [], [], [], [], []
    ks_l, vs_l, cs_l, hs_l, srs_l, sis_l = [], [], [], [], [], []
    for l in range(DEPTH):
        lw = (w_in[l], conv_w[l], conv_b[l], rg_w_a[l], rg_b_a[l], rg_w_x[l], rg_b_x[l], rg_lambda[l],
              s5_lambda_re[l], s5_lambda_im[l], s5_b_re[l], s5_b_im[l], s5_c_re[l], s5_c_im[l], s5_d[l],
              s5_log_step[l], s5_w_glu[l], s5_b_glu[l], w_br_rnn[l], w_br_ssm[l], w_br_attn[l], w_out[l],
              ln1_g[l], ln1_b[l], w_ffn_in[l], w_ffn_out[l], ln2_g[l], ln2_b[l])
        yp, kr, vr, cn, hn, s5r, s5i = trunk_layer(
            yp, 0,
            jnp.zeros((bp, 0, ATTN_HEADS, HEAD_DIM), dt), jnp.zeros((bp, 0, ATTN_HEADS, HEAD_DIM), dt),
            jnp.zeros((bp, CONV_WIDTH - 1, D_RNN), dt), jnp.zeros((bp, D_RNN), dt),
            jnp.zeros((bp, SSM_GROUPS, SSM_STATE), dt), jnp.zeros((bp, SSM_GROUPS, SSM_STATE), dt), *lw)
        kp_l.append(kr); vp_l.append(vr); cp_l.append(cn); hp_l.append(hn); srp_l.append(s5r); sip_l.append(s5i)
        pk = cache_k[l][page_table].reshape(bs, n_pages * PAGE_SIZE, ATTN_HEADS, HEAD_DIM)
        pv = cache_v[l][page_table].reshape(bs, n_pages * PAGE_SIZE, ATTN_HEADS, HEAD_DIM)
        ys, kr, vr, cn, hn, s5r, s5i = trunk_layer(
            ys, PAST_LEN, pk, pv, state_conv[l], state_rglru[l], state_s5_re[l], state_s5_im[l], *lw)
        ks_l.append(kr); vs_l.append(vr); cs_l.append(cn); hs_l.append(hn); srs_l.append(s5r); sis_l.append(s5i)
    return (yp, ys,
            jnp.stack(kp_l), jnp.stack(vp_l), jnp.stack(cp_l), jnp.stack(hp_l), jnp.stack(srp_l), jnp.stack(sip_l),
            jnp.stack(ks_l), jnp.stack(vs_l), jnp.stack(cs_l), jnp.stack(hs_l), jnp.stack(srs_l), jnp.stack(sis_l))
```

```python
import contextlib
import numpy as np
import concourse.bass as bass
import concourse.mybir as mybir
from concourse.bass_utils import run_bass_kernel_spmd

F32 = mybir.dt.float32
BF16 = mybir.dt.bfloat16
I32 = mybir.dt.int32
ALU = mybir.AluOpType
AF = mybir.ActivationFunctionType
AX = mybir.AxisListType

D = 1024
SEQ = 8192
DEPTH = 2
NIN = 6144
DFF = 2816
T = 256
BIG = 30000.0
ALPHA = (2.0 * DEPTH) ** 0.25
LN_EPS = 1e-5
NCST = 1152

PV = {}
_o = 0
for _n, _c in [("cw0", 8), ("cw1", 8), ("cw2", 8), ("cw3", 8), ("cb", 8), ("ba", 8), ("bx", 8), ("lam", 8),
               ("s5d", 4), ("bglu", 4), ("g1", 8), ("b1", 8), ("g2", 8), ("b2", 8)]:
    PV[_n] = _o
    _o += _c
NPV = _o


class Buf:
    __slots__ = ("name", "t", "lw", "rd", "dsem", "dcnt", "excl")

    def __init__(self, name, t):
        self.name = name
        self.t = t
        self.lw = None
        self.rd = {}
        self.dsem = None
        self.dcnt = 0
        self.excl = False


class KB:
    def __init__(self, nc, es):
        self.nc = nc
        self.es = es
        self.E = {"pe": nc.tensor, "act": nc.scalar, "dve": nc.vector, "pool": nc.gpsimd, "sp": nc.sync}
        self.sem = {k: es.enter_context(nc.semaphore("s_" + k)) for k in ["pe", "act", "dve", "pool"]}
        self.cnt = {k: 0 for k in self.sem}
        self.waited = {k: {} for k in self.E}
        self.nbuf = 0
        self.semlatest = {}
        self.ninst = 0

    def sb(self, name, shape, dt):
        t = self.es.enter_context(self.nc.sbuf_tensor("sb_" + name, list(shape), dt))
        return Buf(name, t)

    def ps(self, name, shape, dt=F32):
        t = self.es.enter_context(self.nc.psum_tensor("ps_" + name, list(shape), dt))
        b = Buf(name, t)
        b.excl = True
        return b

    def dram(self, name, shape, dt, kind="Internal"):
        t = self.nc.dram_tensor(name, list(shape), dt, kind=kind)
        b = Buf(name, t)
        return b

    def _wait(self, eng, tickets):
        w = self.waited[eng]
        for (key, h, val) in tickets:
            if key in self.semlatest:
                val = self.semlatest[key]
            if w.get(key, 0) >= val:
                continue
            self.E[eng].wait_ge(h, val)
            w[key] = val

    def _deps(self, eng, reads, writes):
        tk = []
        for b in reads:
            if b.lw is not None:
                tk.append(b.lw)
            if b.excl:
                tk.extend(t for t in b.rd.values() if t[0] != eng)
        for b in writes:
            if b.lw is not None:
                tk.append(b.lw)
            tk.extend(b.rd.values())
        if eng == "pe":
            tk = [t for t in tk if t[0] != "pe"]
        return tk

    def _mark(self, tk, reads, writes):
        for b in reads:
            old = b.rd.get(tk[0])
            if old is None or old[2] < tk[2]:
                b.rd[tk[0]] = tk
        for b in writes:
            b.lw = tk
            b.rd = {}

    def op(self, eng, fn, reads=(), writes=(), inc=True):
        self._wait(eng, self._deps(eng, reads, writes))
        ins = fn(self.E[eng])
        self.ninst += 1
        if inc:
            self.cnt[eng] += 1
            ins.then_inc(self.sem[eng], 1)
            tk = (eng, self.sem[eng], self.cnt[eng])
        else:
            tk = (eng, self.sem[eng], self.cnt[eng] + 1)
        self._mark(tk, reads, writes)
        return tk

    def dma(self, q, out, in_, reads=(), writes=(), owner=None, **kw):
        self._wait(q, self._deps(q, reads, writes))
        b = owner
        if b.dsem is None:
            b.dsem = {}
            b.dcnt = {}
        if q not in b.dsem:
            b.dsem[q] = self.es.enter_context(self.nc.semaphore("d_%s_%s" % (b.name, q)))
            b.dcnt[q] = 0
        b.dcnt[q] += 16
        self.E[q].dma_start(out=out, in_=in_, **kw).then_inc(b.dsem[q], 16)
        self.ninst += 1
        tk = ("d_%s_%s" % (b.name, q), b.dsem[q], b.dcnt[q])
        self.semlatest[tk[0]] = tk[2]
        self._mark(tk, reads, writes)
        return tk

    def dma_ind(self, owner, out, in_, idx_ap, reads=(), writes=None):
        q = "pool"
        writes = [owner] if writes is None else writes
        self._wait(q, self._deps(q, reads, writes))
        b = owner
        if b.dsem is None:
            b.dsem = {}
            b.dcnt = {}
        key = "ind"
        if key not in b.dsem:
            b.dsem[key] = self.es.enter_context(self.nc.semaphore("d_%s_ind" % b.name))
            b.dcnt[key] = 0
        b.dcnt[key] += 16
        self.nc.gpsimd.indirect_dma_start(out=out, out_offset=None, in_=in_,
                                          in_offset=bass.IndirectOffsetOnAxis(ap=idx_ap, axis=0)).then_inc(b.dsem[key], 16)
        self.ninst += 1
        tk = ("d_%s_ind" % b.name, b.dsem[key], b.dcnt[key])
        self.semlatest[tk[0]] = tk[2]
        self._mark(tk, reads, writes)
        return tk

    def wait_all(self, eng, bufs):
        tk = []
        for b in bufs:
            if b.lw is not None:
                tk.append(b.lw)
            tk.extend(b.rd.values())
        self._wait(eng, tk)


def build(cfg):
    SEQ = cfg.get("seq", 8192)
    NCH = cfg.get("nch", SEQ // T)
    NL = cfg.get("nl", DEPTH)
    STAGE = cfg.get("stage", 99)
    nc = bass.Bass("TRN2", target_bir_lowering=False)
    es = contextlib.ExitStack()
    kb = KB(nc, es)
    declared = []

    def din(name, shape, dt=F32):
        declared.append(name)
        return kb.dram(name, shape, dt, kind="ExternalInput")

    def dout(name, shape, dt=F32):
        return kb.dram(name, shape, dt, kind="ExternalOutput")

    xT = din("xT", [D, SEQ])
    w_in = din("w_in", [DEPTH, D, NIN])
    w_bra = din("w_br_rnn", [DEPTH, D, D])
    w_brs = din("w_br_ssm", [DEPTH, 512, D])
    w_brc = din("w_br_attn", [DEPTH, 512, D])
    w_out = din("w_out", [DEPTH, D, D])
    w_f1 = din("w_ffn_in", [DEPTH, D, 2 * DFF])
    w_f2 = din("w_ffn_out", [DEPTH, DFF, D])
    w_glu = din("s5_w_glu", [DEPTH, 512, 512])
    rgwa = din("rgwa", [DEPTH, 8, 128, 128])
    rgwx = din("rgwx", [DEPTH, 8, 128, 128])
    bret = din("bret", [DEPTH, 16, 128, 128])
    bimt = din("bimt", [DEPTH, 16, 128, 128])
    cret = din("cret", [DEPTH, 16, 128, 128])
    cimt = din("cimt", [DEPTH, 16, 128, 128])
    pv_d = din("pv", [DEPTH, 128, NPV])
    ps5_d = din("ps5", [DEPTH, 128, 48])
    ropec_d = din("ropec", [128, SEQ])
    ropes_d = din("ropes", [128, SEQ])
    cst_d = din("cst", [128, NCST])
    etab_d = din("etab", [128, 32 * 128])

    SAMPLE = cfg.get("sample", True)
    if SAMPLE:
        xsT = din("xsT", [D, 128])
        ropecS_d = din("ropecS", [128, T])
        ropesS_d = din("ropesS", [128, T])
        convS_d = din("convS", [DEPTH, 128, 8, 16, 3])
        rgS_d = din("rgS", [DEPTH, 128, 8, 16])
        s5S_d = din("s5S", [DEPTH, 128, 16, 16, 2])
        ptab_d = din("ptab", [128, 256], I32)
        NPOOL = cfg.get("npool", 2560)
        cacheK = din("cache_k", [DEPTH * NPOOL * 128, 512])
        cacheV = din("cache_v", [DEPTH * NPOOL * 128, 512])
        cmaskS_d = din("cmaskS", [128, 128])
        ysT = dout("ysT", [D, 128])
        newkTs = dout("newkTs", [DEPTH, 512, 128])
        newvs = dout("newvs", [DEPTH, 128, 512])
        convs_o = dout("convs_o", [DEPTH, 128, 8, 16, 3])
        rglrus_o = dout("rglrus_o", [DEPTH, 128, 8, 16])
        s5s_o = dout("s5s_o", [DEPTH, 128, 16, 16, 2])
        x1sT = kb.dram("x1sT", [D, 128], F32)
    yT = dout("yT", [D, SEQ])
    newkT = dout("newkT", [DEPTH, 512, SEQ])
    newv = dout("newv", [DEPTH, SEQ, 512])
    conv_o = dout("conv_o", [DEPTH, 128, 8, 3])
    rglru_o = dout("rglru_o", [DEPTH, 128, 8])
    s5_o = dout("s5_o", [DEPTH, 128, 16, 2])

    wb_in = kb.dram("wb_in", [DEPTH, D, NIN], BF16)
    wb_bra = kb.dram("wb_bra", [DEPTH, D, D], BF16)
    wb_brs = kb.dram("wb_brs", [DEPTH, 512, D], BF16)
    wb_brc = kb.dram("wb_brc", [DEPTH, 512, D], BF16)
    wb_out = kb.dram("wb_out", [DEPTH, D, D], BF16)
    wb_f1 = kb.dram("wb_f1", [DEPTH, D, 2 * DFF], BF16)
    wb_f2 = kb.dram("wb_f2", [DEPTH, DFF, D], BF16)
    Ks = [kb.dram("Ks%d" % l, [512, SEQ], BF16) for l in range(DEPTH)]
    Vs = [kb.dram("Vs%d" % l, [8, 128, SEQ // 128, 65], BF16) for l in range(DEPTH)]
    x1T = kb.dram("x1T", [D, SEQ], F32)

    es2 = contextlib.ExitStack()
    stg = []
    for i in range(2):
        t_ = es2.enter_context(nc.sbuf_tensor("sb_stg%d" % i, [128, 12288], BF16))
        stg.append(Buf("stg%d" % i, t_))
    sctr = [0]

    def cast_w(src, dst, rows, cols, l):
        nk_all = rows // 128
        per = max(1, 12288 // cols)
        k = 0
        while k < nk_all:
            nk = min(per, nk_all - k)
            sb_ = stg[sctr[0] % 2]
            sctr[0] += 1
            view = sb_.t[:, 0:nk * cols].rearrange("p (kt n) -> p kt n", kt=nk)
            kb.dma("pool", view, src.t.ap()[l, k * 128:(k + nk) * 128, :].rearrange("(kt p) n -> p kt n", p=128),
                   reads=[src], writes=[sb_], owner=sb_)
            kb.dma("sp", dst.t.ap()[l, k * 128:(k + nk) * 128, :].rearrange("(kt p) n -> p kt n", p=128), view,
                   reads=[sb_], writes=[dst], owner=sb_)
            k += nk

    for l in range(NL):
        cast_w(w_in, wb_in, D, NIN, l)
        if STAGE >= 2:
            cast_w(w_bra, wb_bra, D, D, l)
            cast_w(w_brs, wb_brs, 512, D, l)
            cast_w(w_brc, wb_brc, 512, D, l)
            cast_w(w_out, wb_out, D, D, l)
            cast_w(w_f1, wb_f1, D, 2 * DFF, l)
            cast_w(w_f2, wb_f2, DFF, D, l)
    for e_ in ["pe", "act", "dve", "pool", "sp"]:
        kb.wait_all(e_, stg)
    es2.close()

    cst = kb.sb("cst", [128, NCST], F32)
    identb = kb.sb("identb", [128, 128], BF16)
    causb = kb.sb("causb", [128, 2, T], BF16)
    etab = kb.sb("etab", [128, 32 * 128], BF16)
    pv = kb.sb("pv", [128, NPV], F32)
    ps5 = kb.sb("ps5", [128, 48], F32)
    xf = kb.sb("xf", [128, 8, T], F32)
    xb = kb.sb("xb", [128, 8, T], BF16)
    wsl = [kb.sb("wsl%d" % i, [128, 8, 512], BF16) for i in range(3)]
    wctr = [0]
    ropec = kb.sb("ropec", [128, T], F32)
    ropes = kb.sb("ropes", [128, T], F32)
    xr = kb.sb("xr", [128, 8, 3 + T], F32)
    usf = kb.sb("usf", [128, 4, T], F32)
    usb = kb.sb("usb", [128, 4, T], BF16)
    qb = kb.sb("qb", [128, 4, T], BF16)
    kf = kb.sb("kf", [128, 4, T], F32)
    kbf = kb.sb("kbf", [128, 4, T], BF16)
    vtok = [kb.sb("vtok%d" % i, [128, 512], F32) for i in range(2)]
    vaug = [kb.sb("vaug%d" % i, [128, 8, 65], BF16) for i in range(2)]
    NTMP = 10
    tmp = [kb.sb("tmp%d" % i, [128, T], F32) for i in range(NTMP)]
    tctr = [0]
    psb = [kb.ps("psb%d" % i, [128, 512], F32) for i in range(8)]
    psctr = [0]
    rgwab = kb.sb("rgwab", [128, 8, 128], BF16)
    rgwxb = kb.sb("rgwxb", [128, 8, 128], BF16)
    bretb = kb.sb("bretb", [128, 16, 128], BF16)
    bimtb = kb.sb("bimtb", [128, 16, 128], BF16)
    cretb = kb.sb("cretb", [128, 16, 128], BF16)
    cimtb = kb.sb("cimtb", [128, 16, 128], BF16)
    wglub = kb.sb("wglub", [128, 4, 512], BF16)
    clam = kb.sb("clam", [128, 8], F32)
    carryA = kb.sb("carryA", [128, 8], F32)
    cS5 = kb.sb("cS5", [128, 16, 2], F32)
    s5p = kb.sb("s5p", [128, 8, 16], F32)
    CT = kb.sb("CT", [128, 16, 128], F32)
    ST = kb.sb("ST", [128, 16, 128], F32)
    MRE = kb.sb("MRE", [128, 16, 128], F32)
    MIM = kb.sb("MIM", [128, 16, 128], F32)
    RT = kb.sb("RT", [128, 16, 128], F32)
    ya = kb.sb("ya", [128, 8, T], BF16)
    ys = kb.sb("ys", [128, 4, T], BF16)
    yc = kb.sb("yc", [128, 4, T], BF16)
    gS5f = kb.sb("gS5f", [128, 4, T], F32)
    gS5b = kb.sb("gS5b", [128, 4, T], BF16)
    hreb = kb.sb("hreb", [128, T], BF16)
    himb = kb.sb("himb", [128, T], BF16)
    mg = kb.sb("mg", [128, 8, T], BF16)
    hff = kb.sb("hff", [128, 22, T], BF16)
    KM = kb.sb("KM", [128, 4, 32], BF16)
    kmax2 = kb.sb("kmax2", [1, 8], F32)
    MB = [kb.sb("MB%d" % h, [128, T], BF16) for h in range(8)]
    selb = [kb.sb("selb%d" % i, [128, 8, 32], F32) for i in range(2)]
    gsb = kb.sb("gsb", [128, 8, 8], F32)
    top8 = kb.sb("top8", [128, 8, 8], F32)
    kbuf = [kb.sb("kbuf%d" % i, [128, 2048], BF16) for i in range(2)]
    vbuf = [kb.sb("vbuf%d" % i, [128, 16, 65], BF16) for i in range(2)]
    Pt = [kb.sb("Pt%d" % i, [128, T], BF16) for i in range(2)]
    Osb = kb.sb("Osb", [65, T], F32)
    small = kb.sb("small", [128, 16], F32)
    kvctr = [0, 0, 0]
    if SAMPLE:
        hff32 = hff.t[:].rearrange("p a b -> p (a b)").bitcast(F32)
        xe_v = hff32[:, 0:1408].rearrange("p (m s j) -> p m s j", m=8, s=16)
        s5s0_v = hff32[:, 1408:1920].rearrange("p (j s r) -> p j s r", j=16, s=16)
        s5so_v = hff32[:, 1920:2432].rearrange("p (j s r) -> p j s r", j=16, s=16)
        h0s_v = hff32[:, 2432:2560].rearrange("p (m s) -> p m s", m=8)
        rgso_v = hff32[:, 2560:2688].rearrange("p (m s) -> p m s", m=8)
        idx = kb.sb("idx", [128, 256], I32)
        cmaskS = kb.sb("cmaskS", [128, 128], BF16)
        MBs = kb.sb("MBs", [128, 64], BF16)
        KMs = kb.sb("KMs", [128, 4, 8], BF16)

    ident = cst.t[:, 0:128]
    rrot = cst.t[:, 128:256]
    tau = cst.t[:, 768:896]
    ones64 = cst.t[:, 896:960]
    onesD = cst.t[:, 960:1088]

    def nps():
        b = psb[psctr[0] % 6]
        psctr[0] += 1
        return b

    def nws():
        b = wsl[wctr[0] % 3]
        wctr[0] += 1
        return b

    def ntmp():
        b = tmp[tctr[0] % NTMP]
        tctr[0] += 1
        return b

    def pvc(name, m):
        return pv.t[:, PV[name] + m:PV[name] + m + 1]

    kb.dma("sp", cst.t[:], cst_d.t.ap(), reads=[cst_d], writes=[cst], owner=cst)
    kb.op("dve", lambda e: e.tensor_copy(out=identb.t[:], in_=cst.t[:, 0:128]), reads=[cst], writes=[identb])
    kb.op("dve", lambda e: e.tensor_copy(out=causb.t[:], in_=cst.t[:, 256:768].rearrange("p (a t) -> p a t", a=2)),
          reads=[cst], writes=[causb])
    kb.dma("pool", etab.t[:], etab_d.t.ap(), reads=[etab_d], writes=[etab], owner=etab)
    for i in range(2):
        kb.op("pool", lambda e, i=i: e.memset(vaug[i].t[:], 1.0), writes=[vaug[i]])

    def load_w(wd, l, k0, nk, c0, ncol, slot=None, soff=0):
        s = slot if slot is not None else nws()
        src = wd.t.ap()[l, k0:k0 + nk * 128, c0:c0 + ncol].rearrange("(kt p) n -> p kt n", p=128)
        kb.dma("sp", s.t[:, 0:nk, soff:soff + ncol], src, reads=[wd], writes=[s], owner=s)
        return s

    def mm_group(ps_ap, psbuf, pairs, extra_reads, first=True, last=True):
        n = len(pairs)
        for i, (a, b) in enumerate(pairs):
            st = first and i == 0
            sp_ = last and i == n - 1
            kb.op("pe", lambda e, a=a, b=b, st=st, sp_=sp_: e.matmul(ps_ap, lhsT=a, rhs=b, start=st, stop=sp_),
                  reads=extra_reads, writes=[psbuf], inc=(i == n - 1))

    def rope_tile(ps, dst_f32_ap, dstbuf):
        xs, t1, t2 = ntmp(), ntmp(), ntmp()
        kb.op("act", lambda e: e.copy(out=xs.t[:], in_=ps.t[:, 0:T]), reads=[ps], writes=[xs])
        p2 = nps()
        kb.op("pe", lambda e: e.matmul(p2.t[:, 0:T], lhsT=rrot, rhs=xs.t[:], start=True, stop=True),
              reads=[cst, xs], writes=[p2])
        kb.op("dve", lambda e: e.tensor_tensor(out=t1.t[:], in0=xs.t[:], in1=ropec.t[:], op=ALU.mult),
              reads=[xs, ropec], writes=[t1])
        kb.op("dve", lambda e: e.tensor_tensor(out=t2.t[:], in0=p2.t[:, 0:T], in1=ropes.t[:], op=ALU.mult),
              reads=[p2, ropes], writes=[t2])
        kb.op("dve", lambda e: e.tensor_tensor(out=dst_f32_ap, in0=t1.t[:], in1=t2.t[:], op=ALU.add),
              reads=[t1, t2], writes=[dstbuf])

    def TT(eng, out_ap, outb, a_ap, ab, b_ap, bb, op):
        return kb.op(eng, lambda e: e.tensor_tensor(out=out_ap, in0=a_ap, in1=b_ap, op=op),
                     reads=[ab, bb], writes=[outb])

    def TS(eng, out_ap, outb, a_ap, ab, s1, s2, op0, op1=None, extra=()):
        if op1 is None:
            return kb.op(eng, lambda e: e.tensor_scalar(out=out_ap, in0=a_ap, scalar1=s1, scalar2=None, op0=op0),
                         reads=[ab] + list(extra), writes=[outb])
        return kb.op(eng, lambda e: e.tensor_scalar(out=out_ap, in0=a_ap, scalar1=s1, scalar2=s2, op0=op0, op1=op1),
                     reads=[ab] + list(extra), writes=[outb])

    def STT(out_ap, outb, a_ap, ab, sc, b_ap, bb, op0, op1, extra=()):
        return kb.op("dve", lambda e: e.scalar_tensor_tensor(out=out_ap, in0=a_ap, scalar=sc, in1=b_ap, op0=op0, op1=op1),
                     reads=[ab, bb] + list(extra), writes=[outb])

    def ACT(out_ap, outb, in_ap, inb, func, bias=None, scale=None, extra=()):
        kw = {}
        if bias is not None:
            kw["bias"] = bias
        if scale is not None:
            kw["scale"] = scale
        return kb.op("act", lambda e: e.activation(out=out_ap, in_=in_ap, func=func, **kw),
                     reads=[inb] + list(extra), writes=[outb])

    def layer_norm(gname, bname):
        mps = nps()
        mm_group(mps.t[:, 0:T], mps, [(onesD, xf.t[:, m, :]) for m in range(8)], [cst, xf])
        for m in range(8):
            TT("dve", xf.t[:, m, :], xf, xf.t[:, m, :], xf, mps.t[:, 0:T], mps, ALU.subtract)
        vps = nps()
        for m in range(8):
            sq = ntmp()
            ACT(sq.t[:], sq, xf.t[:, m, :], xf, AF.Square)
            kb.op("pe", lambda e, m=m, sq=sq: e.matmul(vps.t[:, 0:T], lhsT=onesD, rhs=sq.t[:], start=(m == 0), stop=(m == 7)),
                  reads=[cst, sq], writes=[vps], inc=True)
        sd = ntmp()
        rs = ntmp()
        ACT(sd.t[:], sd, vps.t[:, 0:T], vps, AF.Sqrt, bias=epsc, extra=[small])
        kb.op("dve", lambda e: e.reciprocal(out=rs.t[:], in_=sd.t[:]), reads=[sd], writes=[rs])
        for m in range(8):
            TT("dve", xf.t[:, m, :], xf, xf.t[:, m, :], xf, rs.t[:], rs, ALU.mult)
            TS("dve", xf.t[:, m, :], xf, xf.t[:, m, :], xf, pvc(gname, m), pvc(bname, m), ALU.mult, ALU.add, extra=[pv])
        kb.op("act", lambda e: e.copy(out=xb.t[:], in_=xf.t[:]), reads=[xf], writes=[xb])

    kb.op("pool", lambda e: e.memset(small.t[:, 0:1], LN_EPS), writes=[small])
    epsc = small.t[:, 0:1]

    if SAMPLE:
        kb.dma("sp", idx.t[:], ptab_d.t.ap(), reads=[ptab_d], writes=[idx], owner=idx)
        idf = ntmp()
        kb.op("dve", lambda e: e.tensor_copy(out=idf.t[:], in_=idx.t[:]), reads=[idx], writes=[idf])
        TS("dve", idf.t[:], idf, idf.t[:], idf, 128.0, cst.t[:, 1088:1089], ALU.mult, ALU.add, extra=[cst])
        kb.op("dve", lambda e: e.tensor_copy(out=idx.t[:], in_=idf.t[:]), reads=[idf], writes=[idx])
        kb.dma("pool", cmaskS.t[:], cmaskS_d.t.ap(), reads=[cmaskS_d], writes=[cmaskS], owner=cmaskS)
        kb.op("pool", lambda e: e.memset(MBs.t[:], 0.0), writes=[MBs])
        small2 = kb.sb("small2", [128, 32], F32)

    def sample_attention(l):
        NR = NPOOL * 128
        ck = cacheK.t.ap()
        cv = cacheV.t.ap()
        if l > 0:
            idf = ntmp()
            kb.op("dve", lambda e: e.tensor_copy(out=idf.t[:], in_=idx.t[:]), reads=[idx], writes=[idf])
            TS("dve", idf.t[:], idf, idf.t[:], idf, float(NR), None, ALU.add)
            kb.op("dve", lambda e: e.tensor_copy(out=idx.t[:], in_=idf.t[:]), reads=[idf], writes=[idx])
        Kpg = vtok[1]
        usf_flat = usf.t[:].rearrange("p a b -> p (a b)")
        Vpg = usf_flat[:, 0:512]
        Ksq = usf_flat[:, 512:1024]
        Oacc = [psb[6], psb[7]]
        kn2m = gsb.t[:, 0, :]
        kn2p = gsb.t[:, 1, :]
        kb.op("pool", lambda e: e.memset(yc.t[:, :, 128:T], 0.0), writes=[yc])
        for h in range(8):
            kb.op("pool", lambda e, h=h: e.memset(MB[h].t[0:32, :], 0.0), writes=[MB[h]])
        for hp in range(4):
            ksq = ntmp()
            ACT(ksq.t[:], ksq, kf.t[:, hp, :], kf, AF.Square)
            ACT(gS5f.t[:, hp, :], gS5f, qb.t[:, hp, :], qb, AF.Square)
            for hh in range(2):
                h = hp * 2 + hh
                b0 = hh * 64
                pk = nps()
                kb.op("pe", lambda e: e.matmul(pk.t[0:1, 0:128], lhsT=ones64[b0:b0 + 64, 0:1], rhs=ksq.t[b0:b0 + 64, 0:128],
                                               start=True, stop=True), reads=[cst, ksq], writes=[pk])
                kb.op("dve", lambda e: e.tensor_reduce(out=kmax2.t[0:1, h:h + 1], in_=pk.t[0:1, 0:128], axis=AX.X, op=ALU.max),
                      reads=[pk], writes=[kmax2])
        for o_ in Oacc:
            kb.op("dve", lambda e, o_=o_: e.memset(o_.t[0:65, :], 0.0), writes=[o_])
        for sq_ in range(16):
            qs_ = slice(sq_ * 8, sq_ * 8 + 8)
            kb.op("pool", lambda e: e.memset(kn2m, 0.0), writes=[gsb])
            for j in range(16):
                col = sq_ * 16 + j
                kb.dma_ind(Kpg, Kpg.t[:], ck, idx.t[:, col:col + 1], reads=[idx, cacheK])
                tp = nps()
                for hp in range(4):
                    kb.op("pe", lambda e, hp=hp: e.transpose(out=tp.t[:, hp * 128:(hp + 1) * 128],
                                                             in_=Kpg.t[:, hp * 128:(hp + 1) * 128], identity=ident),
                          reads=[Kpg, cst], writes=[tp], inc=(hp == 3))
                ws_ = wsl[j // 8]
                kb.op("act", lambda e: e.copy(out=ws_.t[:, j % 8, :], in_=tp.t[:, :]), reads=[tp], writes=[ws_])
                half_ = small2.t[:, (j % 2) * 4:(j % 2) * 4 + 4]
                kb.op("dve", lambda e: e.tensor_reduce(out=half_, in_=tp.t[:, :].rearrange("p (h t) -> p h t", h=4),
                                                       axis=AX.X, op=ALU.add), reads=[tp], writes=[small2])
                if j % 2 == 1:
                    TT("dve", small2.t[:, 0:4], small2, small2.t[:, 0:4], small2, small2.t[:, 4:8], small2, ALU.add)
                    kb.op("act", lambda e: e.mul(out=KMs.t[:, :, j // 2], in_=small2.t[:, 0:4], mul=1.0 / 256.0),
                          reads=[small2], writes=[KMs])
                ACT(Ksq, usf, Kpg.t[:], Kpg, AF.Square)
                kb.op("dve", lambda e: e.tensor_reduce(out=kn2p, in_=Ksq.rearrange("p (h d) -> p h d", h=8), axis=AX.X, op=ALU.add),
                      reads=[usf], writes=[gsb])
                TT("dve", kn2m, gsb, kn2m, gsb, kn2p, gsb, ALU.max)
            Gsb = [nps(), nps()]
            for h in range(8):
                hp, b0 = h // 2, (h % 2) * 64
                kb.op("pe", lambda e: e.matmul(Gsb[h % 2].t[0:8, (h // 2) * 8:(h // 2 + 1) * 8], lhsT=qb.t[b0:b0 + 64, hp, qs_],
                                               rhs=KMs.t[b0:b0 + 64, hp, 0:8], start=True, stop=True),
                      reads=[qb, KMs], writes=[Gsb[h % 2]], inc=(h >= 6))
            for h in range(8):
                gsrc = Gsb[h % 2].t[0:8, (h // 2) * 8:(h // 2 + 1) * 8]
                kb.op("dve", lambda e: e.max(out=top8.t[0:8, h, :], in_=gsrc), reads=[Gsb[h % 2]], writes=[top8])
                TS("dve", selb[0].t[0:8, h, 0:8], selb[0], gsrc, Gsb[h % 2], top8.t[0:8, h, 2:3], -BIG,
                   ALU.is_lt, ALU.mult, extra=[top8])
            tpm = nps()
            for h in range(8):
                kb.op("pe", lambda e: e.transpose(out=tpm.t[0:8, h * 8:(h + 1) * 8], in_=selb[0].t[0:8, h, 0:8],
                                                  identity=ident[0:8, 0:8]), reads=[selb[0], cst], writes=[tpm], inc=(h == 7))
            kb.op("act", lambda e: e.copy(out=MBs.t[0:8, :], in_=tpm.t[0:8, 0:64]), reads=[tpm], writes=[MBs])
            tk_ = nps()
            kb.op("pe", lambda e: e.transpose(out=tk_.t[0:8, 0:128], in_=kn2m, identity=ident), reads=[gsb, cst], writes=[tk_])
            kb.op("dve", lambda e: e.tensor_reduce(out=small2.t[0:8, 8:9], in_=tk_.t[0:8, 0:128], axis=AX.X, op=ALU.max),
                  reads=[tk_], writes=[small2])
            tk2 = nps()
            kb.op("pe", lambda e: e.transpose(out=tk2.t[0:1, 0:8], in_=small2.t[0:8, 8:9], identity=ident[0:8, 0:8]),
                  reads=[small2, cst], writes=[tk2])
            TT("dve", small2.t[0:1, 16:24], small2, tk2.t[0:1, 0:8], tk2, kmax2.t[0:1, 0:8], kmax2, ALU.max)
            pqb = [nps(), nps()]
            for h in range(8):
                hp, b0 = h // 2, (h % 2) * 64
                kb.op("pe", lambda e: e.matmul(pqb[h % 2].t[0:1, (h // 2) * 8:(h // 2 + 1) * 8], lhsT=ones64[b0:b0 + 64, 0:1],
                                               rhs=gS5f.t[b0:b0 + 64, hp, qs_], start=True, stop=True),
                      reads=[cst, gS5f], writes=[pqb[h % 2]], inc=(h >= 6))
            mr = ntmp()
            for h in range(8):
                TS("dve", mr.t[0:1, h * 8:(h + 1) * 8], mr, pqb[h % 2].t[0:1, (h // 2) * 8:(h // 2 + 1) * 8], pqb[h % 2],
                   small2.t[0:1, 16 + h:17 + h], None, ALU.mult, extra=[small2])
            ACT(mr.t[0:1, 0:64], mr, mr.t[0:1, 0:64], mr, AF.Sqrt)
            TS("dve", MBs.t[32:33, :], MBs, mr.t[0:1, 0:64], mr, -1.02, -0.5, ALU.mult, ALU.add)
            for h in range(8):
                kb.op("act", lambda e: e.copy(out=MB[h].t[32:33, qs_], in_=MBs.t[32:33, h * 8:(h + 1) * 8]),
                      reads=[MBs], writes=[MB[h]])
            for j in range(16):
                col = sq_ * 16 + j
                n = j // 2
                kb.dma_ind(usf, Vpg, cv, idx.t[:, col:col + 1], reads=[idx, cacheV])
                va = vaug[1]
                kb.op("dve", lambda e: e.tensor_copy(out=va.t[:, :, 0:64], in_=Vpg.rearrange("p (h d) -> p h d", h=8)),
                      reads=[usf], writes=[va])
                Sb = [nps(), nps()]
                ws_ = wsl[j // 8]
                for h in range(8):
                    hp, b0 = h // 2, (h % 2) * 64
                    S = Sb[h % 2]
                    sc_ = slice((h // 2) * 8, (h // 2 + 1) * 8)
                    kb.op("pe", lambda e: e.matmul(S.t[:, sc_], lhsT=ws_.t[b0:b0 + 64, j % 8, hp * 128:(hp + 1) * 128],
                                                   rhs=qb.t[b0:b0 + 64, hp, qs_], start=True, stop=False),
                          reads=[ws_, qb], writes=[S], inc=False)
                    kb.op("pe", lambda e: e.matmul(S.t[:, sc_], lhsT=etab.t[:, n * 128:(n + 1) * 128],
                                                   rhs=MBs.t[:, h * 8:(h + 1) * 8], start=False, stop=True),
                          reads=[etab, MBs], writes=[S], inc=(h >= 6))
                P_ = Pt[kvctr[2] % 2]
                kvctr[2] += 1
                for par in range(2):
                    kb.op("act", lambda e, par=par: e.activation(
                        out=P_.t[:, 0:64].rearrange("p (a b q) -> p a b q", b=2, q=8)[:, :, par, :],
                        in_=Sb[par].t[:, 0:32].rearrange("p (a q) -> p a q", q=8), func=AF.Exp),
                        reads=[Sb[par]], writes=[P_])
                for h in range(8):
                    o_ = Oacc[h // 4]
                    c0_ = (h % 4) * 128 + sq_ * 8
                    kb.op("pe", lambda e: e.matmul(o_.t[0:65, c0_:c0_ + 8], lhsT=va.t[:, h, :], rhs=P_.t[:, h * 8:(h + 1) * 8],
                                                   start=False, stop=False, skip_group_check=True),
                          reads=[va, P_], writes=[o_], inc=(h == 7))
        for h in range(8):
            hp, b0 = h // 2, (h % 2) * 64
            S = nps()
            kb.op("pe", lambda e: e.matmul(S.t[:, 0:128], lhsT=kbf.t[b0:b0 + 64, hp, 0:128], rhs=qb.t[b0:b0 + 64, hp, 0:128],
                                           start=True, stop=False), reads=[kbf, qb], writes=[S], inc=False)
            kb.op("pe", lambda e: e.matmul(S.t[:, 0:128], lhsT=etab.t[:, 0:128], rhs=MB[h].t[:, 0:128],
                                           start=False, stop=False), reads=[etab, MB[h]], writes=[S], inc=False)
            kb.op("pe", lambda e: e.matmul(S.t[:, 0:128], lhsT=identb.t[:], rhs=cmaskS.t[:], start=False, stop=True),
                  reads=[identb, cmaskS], writes=[S], inc=True)
            P_ = Pt[kvctr[2] % 2]
            kvctr[2] += 1
            ACT(P_.t[:, 0:128], P_, S.t[:, 0:128], S, AF.Exp)
            o_ = Oacc[h // 4]
            c0_ = (h % 4) * 128
            kb.op("pe", lambda e: e.matmul(o_.t[0:65, c0_:c0_ + 128], lhsT=vaug[0].t[:, h, :], rhs=P_.t[:, 0:128],
                                           start=False, stop=True, skip_group_check=True),
                  reads=[vaug[0], P_], writes=[o_], inc=True)
        for h in range(8):
            hp, b0 = h // 2, (h % 2) * 64
            o_ = Oacc[h // 4]
            c0_ = (h % 4) * 128
            kb.op("act", lambda e: e.copy(out=Osb.t[:, 0:128], in_=o_.t[0:65, c0_:c0_ + 128]), reads=[o_], writes=[Osb])
            rden = ntmp()
            kb.op("dve", lambda e: e.reciprocal(out=rden.t[64:65, 0:128], in_=Osb.t[64:65, 0:128]), reads=[Osb], writes=[rden])
            bc = nps()
            kb.op("pe", lambda e: e.matmul(bc.t[0:64, 0:128], lhsT=ones64[64:65, 0:64], rhs=rden.t[64:65, 0:128],
                                           start=True, stop=True), reads=[cst, rden], writes=[bc])
            TT("dve", yc.t[b0:b0 + 64, hp, 0:128], yc, Osb.t[0:64, 0:128], Osb, bc.t[0:64, 0:128], bc, ALU.mult)

    for l in range(NL):
        kb.dma("sp", pv.t[:], pv_d.t.ap()[l], reads=[pv_d], writes=[pv], owner=pv)
        if STAGE >= 2:
            kb.dma("sp", ps5.t[:], ps5_d.t.ap()[l], reads=[ps5_d], writes=[ps5], owner=ps5)
            for (dst, src, n) in [(rgwab, rgwa, 8), (rgwxb, rgwx, 8), (bretb, bret, 16), (bimtb, bimt, 16),
                                  (cretb, cret, 16), (cimtb, cimt, 16)]:
                kb.dma("pool", dst.t[:], src.t.ap()[l].rearrange("j p n -> p j n"), reads=[src], writes=[dst], owner=dst)
            kb.dma("pool", wglub.t[:], w_glu.t.ap()[l].rearrange("(kt p) n -> p kt n", p=128),
                   reads=[w_glu], writes=[wglub], owner=wglub)
            ACT(clam.t[:], clam, pv.t[:, PV["lam"]:PV["lam"] + 8], pv, AF.Exp, scale=-1.0)
            ACT(clam.t[:], clam, clam.t[:], clam, AF.Ln, bias=1.0)
            kb.op("act", lambda e: e.mul(out=clam.t[:], in_=clam.t[:], mul=-8.0), reads=[clam], writes=[clam])
            kb.op("pool", lambda e: e.memset(carryA.t[:], 0.0), writes=[carryA])
            kb.op("pool", lambda e: e.memset(cS5.t[:], 0.0), writes=[cS5])
            kb.op("pool", lambda e: e.memset(xr.t[:, :, 0:3], 0.0), writes=[xr])
            kb.op("pool", lambda e: e.memset(KM.t[:], 0.0), writes=[KM])
            kb.op("pool", lambda e: e.memset(kmax2.t[:], 0.0), writes=[kmax2])
            for i in range(2):
                kb.op("pool", lambda e, i=i: e.memset(selb[i].t[:], 0.0), writes=[selb[i]])
            for h in range(8):
                kb.op("pool", lambda e, h=h: e.memset(MB[h].t[:], 0.0), writes=[MB[h]])
            lre = ps5.t[:, 0:16]
            lim = ps5.t[:, 16:32]
            lst = ps5.t[:, 32:48]
            dtv = s5p.t[:, 0, :]
            mag = s5p.t[:, 1, :]
            th = s5p.t[:, 2, :]
            zr = s5p.t[:, 3, :]
            zi = s5p.t[:, 4, :]
            w1 = s5p.t[:, 5, :]
            w2 = s5p.t[:, 6, :]
            w3 = s5p.t[:, 7, :]
            ACT(dtv, s5p, lst, ps5, AF.Exp)
            TT("dve", w1, s5p, lre, ps5, dtv, s5p, ALU.mult)
            ACT(mag, s5p, w1, s5p, AF.Exp)
            TT("dve", th, s5p, lim, ps5, dtv, s5p, ALU.mult)
            TWO_PI = 2.0 * np.pi
            for quarter in range(4):
                ang = vtok[0]
                wk = vtok[1]
                for jj in range(4):
                    j = quarter * 4 + jj
                    TS("dve", ang.t[:, jj * 128:(jj + 1) * 128], ang, tau, cst, s5p.t[:, 2, j:j + 1], None, ALU.mult,
                       extra=[s5p])
                for (dstT, shift) in [(ST, 0.0), (CT, np.pi / 2)]:
                    TS("dve", wk.t[:], wk, ang.t[:], ang, float(shift), None, ALU.add)
                    ki = ntmp()
                    kiv = ki.t[:].bitcast(I32)
                    for half in range(2):
                        sl = slice(half * 256, (half + 1) * 256)
                        kb.op("dve", lambda e, sl=sl: e.tensor_scalar(out=kiv, in0=wk.t[:, sl], scalar1=1.0 / TWO_PI,
                                                                     scalar2=None, op0=ALU.mult),
                              reads=[wk], writes=[ki])
                        kf_ = ntmp()
                        kb.op("dve", lambda e, kf_=kf_: e.tensor_copy(out=kf_.t[:], in_=kiv), reads=[ki], writes=[kf_])
                        STT(wk.t[:, sl], wk, kf_.t[:], kf_, -TWO_PI, wk.t[:, sl], wk, ALU.mult, ALU.add)
                        m1 = ntmp()
                        TS("dve", m1.t[:], m1, wk.t[:, sl], wk, float(np.pi), -TWO_PI, ALU.is_gt, ALU.mult)
                        TT("dve", wk.t[:, sl], wk, wk.t[:, sl], wk, m1.t[:], m1, ALU.add)
                        m2 = ntmp()
                        TS("dve", m2.t[:], m2, wk.t[:, sl], wk, float(-np.pi), TWO_PI, ALU.is_lt, ALU.mult)
                        TT("dve", wk.t[:, sl], wk, wk.t[:, sl], wk, m2.t[:], m2, ALU.add)
                    ACT(dstT.t[:, quarter * 4:(quarter + 1) * 4, :].rearrange("p j n -> p (j n)"), dstT, wk.t[:], wk, AF.Sin)
            abr, abi, den = w1, w2, w3
            TT("dve", abr, s5p, mag, s5p, CT.t[:, :, 0], CT, ALU.mult)
            TT("dve", abi, s5p, mag, s5p, ST.t[:, :, 0], ST, ALU.mult)
            TS("dve", abr, s5p, abr, s5p, -1.0, None, ALU.add)
            sA = small.t[:, 1:2]
            t_a = ntmp()
            t_b = ntmp()
            ta, tb = t_a.t[:, 0:16], t_b.t[:, 0:16]
            TT("dve", den, s5p, lre, ps5, lre, ps5, ALU.mult)
            TT("dve", ta, t_a, lim, ps5, lim, ps5, ALU.mult)
            TT("dve", den, s5p, den, s5p, ta, t_a, ALU.add)
            kb.op("dve", lambda e: e.reciprocal(out=den, in_=den), reads=[s5p], writes=[s5p])
            TT("dve", ta, t_a, abr, s5p, lre, ps5, ALU.mult)
            TT("dve", tb, t_b, abi, s5p, lim, ps5, ALU.mult)
            TT("dve", ta, t_a, ta, t_a, tb, t_b, ALU.add)
            TT("dve", zr, s5p, ta, t_a, den, s5p, ALU.mult)
            TT("dve", ta, t_a, abi, s5p, lre, ps5, ALU.mult)
            TT("dve", tb, t_b, abr, s5p, lim, ps5, ALU.mult)
            TT("dve", ta, t_a, ta, t_a, tb, t_b, ALU.subtract)
            TT("dve", zi, s5p, ta, t_a, den, s5p, ALU.mult)
            for j in range(16):
                zrj = s5p.t[:, 3, j:j + 1]
                zij = s5p.t[:, 4, j:j + 1]
                TS("dve", MRE.t[:, j, :], MRE, CT.t[:, j, :], CT, zrj, None, ALU.mult, extra=[s5p])
                STT(MRE.t[:, j, :], MRE, ST.t[:, j, :], ST, zij, MRE.t[:, j, :], MRE, ALU.mult, ALU.add, extra=[s5p])
                TS("dve", MIM.t[:, j, :], MIM, ST.t[:, j, :], ST, zrj, None, ALU.mult, extra=[s5p])
                STT(MIM.t[:, j, :], MIM, CT.t[:, j, :], CT, zij, MIM.t[:, j, :], MIM, ALU.mult, ALU.subtract, extra=[s5p])
                TS("dve", RT.t[:, j, :], RT, tau, cst, 0.0, s5p.t[:, 1, j:j + 1], ALU.mult, ALU.add, extra=[s5p])

        xsrc = xT if l == 0 else x1T
        chunks = [("P", c) for c in range(NCH)] + ([("S", 0)] if SAMPLE else [])
        for (mode, c) in chunks:
            SM = (mode == "S")
            t0 = c * T
            if SM:
                if NCH > 0:
                    kb.dma("pool", conv_o.t.ap()[l], xr.t[:, :, T:T + 3], reads=[xr], writes=[conv_o], owner=xr)
                    kb.dma("pool", rglru_o.t.ap()[l], carryA.t[:], reads=[carryA], writes=[rglru_o], owner=carryA)
                    kb.dma("pool", s5_o.t.ap()[l], cS5.t[:], reads=[cS5], writes=[s5_o], owner=cS5)
                xs_src = xsT if l == 0 else x1sT
                kb.op("pool", lambda e: e.memset(xf.t[:, :, 128:T], 0.0), writes=[xf])
                kb.dma("sp", xf.t[:, :, 0:128], xs_src.t.ap().rearrange("(m p) t -> p m t", p=128),
                       reads=[xs_src], writes=[xf], owner=xf)
                kb.op("act", lambda e: e.copy(out=xb.t[:], in_=xf.t[:]), reads=[xf], writes=[xb])
                kb.dma("sp", ropec.t[:], ropecS_d.t.ap(), reads=[ropecS_d], writes=[ropec], owner=ropec)
                kb.dma("sp", ropes.t[:], ropesS_d.t.ap(), reads=[ropesS_d], writes=[ropes], owner=ropes)
                kb.dma("sp", xe_v[:, :, :, 0:3], convS_d.t.ap()[l], reads=[convS_d], writes=[hff], owner=hff)
                kb.dma("sp", h0s_v, rgS_d.t.ap()[l], reads=[rgS_d], writes=[hff], owner=hff)
                kb.dma("sp", s5s0_v, s5S_d.t.ap()[l], reads=[s5S_d], writes=[hff], owner=hff)
            else:
                kb.dma("sp", xf.t[:], xsrc.t.ap()[:, t0:t0 + T].rearrange("(m p) t -> p m t", p=128),
                       reads=[xsrc], writes=[xf], owner=xf)
                kb.op("act", lambda e: e.copy(out=xb.t[:], in_=xf.t[:]), reads=[xf], writes=[xb])
                kb.dma("sp", ropec.t[:], ropec_d.t.ap()[:, t0:t0 + T], reads=[ropec_d], writes=[ropec], owner=ropec)
                kb.dma("sp", ropes.t[:], ropes_d.t.ap()[:, t0:t0 + T], reads=[ropes_d], writes=[ropes], owner=ropes)
                if c > 0:
                    kb.op("pool", lambda e: e.tensor_copy(out=xr.t[:, :, 0:3], in_=xr.t[:, :, T:T + 3]), reads=[xr], writes=[xr])
            for s in cfg.get("strips", range(6)):
                ws = load_w(wb_in, l, 0, 8, s * 512, 512)
                if s == 5:
                    for ts in range(1 if SM else 2):
                        p = nps()
                        mm_group(p.t[:, :], p,
                                 [(xb.t[:, kt, ts * 128:(ts + 1) * 128], ws.t[:, kt, :]) for kt in range(8)],
                                 [xb, ws])
                        vt = vtok[ts]
                        va = vaug[ts]
                        kb.op("act", lambda e, vt=vt, p=p: e.copy(out=vt.t[:], in_=p.t[:]), reads=[p], writes=[vt])
                        kb.op("dve", lambda e, va=va, p=p: e.tensor_copy(
                            out=va.t[:, :, 0:64], in_=p.t[:].rearrange("p (h d) -> p h d", h=8)),
                            reads=[p], writes=[va])
                        tk0 = t0 + ts * 128
                        if SM:
                            kb.dma("pool", newvs.t.ap()[l], vt.t[:], reads=[vt], writes=[newvs], owner=vt)
                        else:
                            kb.dma("pool", newv.t.ap()[l, tk0:tk0 + 128, :], vt.t[:], reads=[vt], writes=[newv], owner=vt)
                            kb.dma("pool", Vs[l].t.ap()[:, :, tk0 // 128, :].rearrange("h p e -> p h e"), va.t[:],
                                   reads=[va], writes=[Vs[l]], owner=va)
                    continue
                for mi in range(4):
                    p = nps()
                    mm_group(p.t[:, 0:T], p,
                             [(ws.t[:, kt, mi * 128:(mi + 1) * 128], xb.t[:, kt, :]) for kt in range(8)], [xb, ws])
                    if s < 2:
                        m = s * 4 + mi
                        if SM:
                            kb.op("act", lambda e, m=m, p=p: e.copy(out=xe_v[:, m, :, 3:11],
                                                                   in_=p.t[:, 0:128].rearrange("p (s t) -> p s t", s=16)),
                                  reads=[p], writes=[hff])
                        else:
                            kb.op("act", lambda e, m=m, p=p: e.copy(out=xr.t[:, m, 3:3 + T], in_=p.t[:, 0:T]),
                                  reads=[p], writes=[xr])
                    elif s == 2:
                        kb.op("act", lambda e, mi=mi, p=p: e.copy(out=usf.t[:, mi, :], in_=p.t[:, 0:T]),
                              reads=[p], writes=[usf])
                        kb.op("act", lambda e, mi=mi: e.copy(out=usb.t[:, mi, :], in_=usf.t[:, mi, :]),
                              reads=[usf], writes=[usb])
                    elif s == 3:
                        tq = ntmp()
                        rope_tile(p, tq.t[:], tq)
                        kb.op("act", lambda e, mi=mi, tq=tq: e.mul(out=qb.t[:, mi, :], in_=tq.t[:], mul=0.125),
                              reads=[tq], writes=[qb])
                    elif s == 4:
                        rope_tile(p, kf.t[:, mi, :], kf)
                        kb.op("act", lambda e, mi=mi: e.copy(out=kbf.t[:, mi, :], in_=kf.t[:, mi, :]),
                              reads=[kf], writes=[kbf])
            if SM:
                kb.dma("pool", newkTs.t.ap()[l].rearrange("(m p) t -> p m t", p=128), kf.t[:, :, 0:128],
                       reads=[kf], writes=[newkTs], owner=kf)
            elif cfg.get("kout", True):
                kb.dma("pool", newkT.t.ap()[l, :, t0:t0 + T].rearrange("(m p) t -> p m t", p=128), kf.t[:],
                       reads=[kf], writes=[newkT], owner=kf)
                kb.dma("pool", Ks[l].t.ap()[:, t0:t0 + T].rearrange("(m p) t -> p m t", p=128), kbf.t[:],
                       reads=[kbf], writes=[Ks[l]], owner=kbf)
            if STAGE < 2:
                continue
            for m in range(8):
                acc = ntmp()
                if SM:
                    kb.op("pool", lambda e: e.memset(acc.t[:, 128:T], 0.0), writes=[acc])
                    accv = acc.t[:, 0:128].rearrange("p (s t) -> p s t", s=16)
                    TS("dve", accv, acc, xe_v[:, m, :, 0:8], hff, pvc("cw0", m), pvc("cb", m), ALU.mult, ALU.add, extra=[pv])
                    for j in range(1, 4):
                        STT(accv, acc, xe_v[:, m, :, j:j + 8], hff, pvc("cw%d" % j, m), accv, acc, ALU.mult, ALU.add,
                            extra=[pv])
                else:
                    TS("dve", acc.t[:], acc, xr.t[:, m, 0:T], xr, pvc("cw0", m), pvc("cb", m), ALU.mult, ALU.add, extra=[pv])
                    for j in range(1, 4):
                        STT(acc.t[:], acc, xr.t[:, m, j:j + T], xr, pvc("cw%d" % j, m), acc.t[:], acc, ALU.mult, ALU.add,
                            extra=[pv])
                xcb = ntmp()
                xcbv = xcb.t[:].bitcast(BF16)[:, 0:T]
                kb.op("act", lambda e: e.copy(out=xcbv, in_=acc.t[:]), reads=[acc], writes=[xcb])
                pa = nps()
                kb.op("pe", lambda e: e.matmul(pa.t[:, 0:T], lhsT=rgwab.t[:, m, :], rhs=xcbv, start=True, stop=True),
                      reads=[rgwab, xcb], writes=[pa])
                px = nps()
                kb.op("pe", lambda e: e.matmul(px.t[:, 0:T], lhsT=rgwxb.t[:, m, :], rhs=xcbv, start=True, stop=True),
                      reads=[rgwxb, xcb], writes=[px])
                r_ = ntmp()
                i_ = ntmp()
                ACT(r_.t[:], r_, pa.t[:, 0:T], pa, AF.Sigmoid, bias=pvc("ba", m), extra=[pv])
                ACT(i_.t[:], i_, px.t[:, 0:T], px, AF.Sigmoid, bias=pvc("bx", m), extra=[pv])
                a_ = ntmp()
                ACT(a_.t[:], a_, r_.t[:], r_, AF.Exp, scale=clam.t[:, m:m + 1], extra=[clam])
                a2 = ntmp()
                TT("dve", a2.t[:], a2, a_.t[:], a_, a_.t[:], a_, ALU.mult)
                ACT(a2.t[:], a2, a2.t[:], a2, AF.Sqrt, bias=1.0, scale=-1.0)
                TT("dve", i_.t[:], i_, i_.t[:], i_, a2.t[:], a2, ALU.mult)
                TT("dve", i_.t[:], i_, i_.t[:], i_, acc.t[:], acc, ALU.mult)
                h_ = ntmp()
                if SM:
                    kb.op("pool", lambda e: e.memset(h_.t[:, 128:T], 0.0), writes=[h_])
                    for sq_ in range(16):
                        cs_ = slice(sq_ * 8, sq_ * 8 + 8)
                        kb.op("dve", lambda e: e.tensor_tensor_scan(out=h_.t[:, cs_], data0=a_.t[:, cs_], data1=i_.t[:, cs_],
                                                                    initial=h0s_v[:, m, sq_:sq_ + 1], op0=ALU.mult, op1=ALU.add),
                              reads=[a_, i_, hff], writes=[h_])
                    kb.op("act", lambda e: e.copy(out=rgso_v[:, m, :], in_=h_.t[:, 0:128].rearrange("p (s t) -> p s t", s=16)[:, :, 7]),
                          reads=[h_], writes=[hff])
                else:
                    kb.op("dve", lambda e: e.tensor_tensor_scan(out=h_.t[:], data0=a_.t[:], data1=i_.t[:],
                                                                initial=carryA.t[:, m:m + 1], op0=ALU.mult, op1=ALU.add),
                          reads=[a_, i_, carryA], writes=[h_])
                    kb.op("act", lambda e: e.copy(out=carryA.t[:, m:m + 1], in_=h_.t[:, T - 1:T]), reads=[h_], writes=[carryA])
                kb.op("act", lambda e: e.copy(out=ya.t[:, m, :], in_=h_.t[:]), reads=[h_], writes=[ya])
            if STAGE < 3:
                continue
            yps = psb[6]
            for mo in range(4):
                for jj in range(4):
                    j = mo * 4 + jj
                    pre = nps()
                    pim = nps()
                    kb.op("pe", lambda e: e.matmul(pre.t[:, 0:T], lhsT=bretb.t[:, j, :], rhs=usb.t[:, mo, :], start=True, stop=True),
                          reads=[bretb, usb], writes=[pre])
                    kb.op("pe", lambda e: e.matmul(pim.t[:, 0:T], lhsT=bimtb.t[:, j, :], rhs=usb.t[:, mo, :], start=True, stop=True),
                          reads=[bimtb, usb], writes=[pim])
                    if SM and jj == 0 and mo == 0:
                        kb.op("pool", lambda e: e.memset(hreb.t[:, 128:T], 0.0), writes=[hreb])
                        kb.op("pool", lambda e: e.memset(himb.t[:, 128:T], 0.0), writes=[himb])
                    for sub in range(1 if SM else 2):
                        cs = slice(sub * 128, (sub + 1) * 128)
                        t1, t2, t3, t4 = ntmp(), ntmp(), ntmp(), ntmp()
                        H = 128
                        if SM:
                            v3 = lambda ap: ap.rearrange("p (s t) -> p s t", s=16)
                            tb = lambda X: X.t[:, j, 0:8].unsqueeze(1).broadcast_to([128, 16, 8])
                        else:
                            v3 = lambda ap: ap
                            tb = lambda X: X.t[:, j, :]
                        lo = lambda b_: v3(b_.t[:, 0:H])
                        hi = lambda b_: v3(b_.t[:, H:2 * H])
                        TT("dve", lo(t1), t1, v3(pre.t[:, cs]), pre, tb(MRE), MRE, ALU.mult)
                        TT("dve", lo(t2), t2, v3(pim.t[:, cs]), pim, tb(MIM), MIM, ALU.mult)
                        TT("dve", lo(t1), t1, lo(t1), t1, lo(t2), t2, ALU.subtract)
                        TT("dve", lo(t3), t3, v3(pim.t[:, cs]), pim, tb(MRE), MRE, ALU.mult)
                        TT("dve", lo(t4), t4, v3(pre.t[:, cs]), pre, tb(MIM), MIM, ALU.mult)
                        TT("dve", lo(t3), t3, lo(t3), t3, lo(t4), t4, ALU.add)
                        if SM:
                            for sq_ in range(16):
                                c0_ = slice(sq_ * 8, sq_ * 8 + 8)
                                c1_ = slice(H + sq_ * 8, H + sq_ * 8 + 8)
                                kb.op("dve", lambda e: e.tensor_tensor_scan(out=t1.t[:, c1_], data0=RT.t[:, j, 0:8], data1=t1.t[:, c0_],
                                                                            initial=s5s0_v[:, j, sq_, 0:1], op0=ALU.mult, op1=ALU.add),
                                      reads=[RT, t1, hff], writes=[t1])
                                kb.op("dve", lambda e: e.tensor_tensor_scan(out=t3.t[:, c1_], data0=RT.t[:, j, 0:8], data1=t3.t[:, c0_],
                                                                            initial=s5s0_v[:, j, sq_, 1:2], op0=ALU.mult, op1=ALU.add),
                                      reads=[RT, t3, hff], writes=[t3])
                        else:
                            kb.op("dve", lambda e: e.tensor_tensor_scan(out=t1.t[:, H:2 * H], data0=RT.t[:, j, :], data1=t1.t[:, 0:H],
                                                                        initial=cS5.t[:, j, 0:1], op0=ALU.mult, op1=ALU.add),
                                  reads=[RT, t1, cS5], writes=[t1])
                            kb.op("dve", lambda e: e.tensor_tensor_scan(out=t3.t[:, H:2 * H], data0=RT.t[:, j, :], data1=t3.t[:, 0:H],
                                                                        initial=cS5.t[:, j, 1:2], op0=ALU.mult, op1=ALU.add),
                                  reads=[RT, t3, cS5], writes=[t3])
                        TT("dve", lo(t2), t2, hi(t1), t1, tb(CT), CT, ALU.mult)
                        TT("dve", hi(t2), t2, hi(t3), t3, tb(ST), ST, ALU.mult)
                        TT("dve", lo(t2), t2, lo(t2), t2, hi(t2), t2, ALU.subtract)
                        TT("dve", lo(t4), t4, hi(t3), t3, tb(CT), CT, ALU.mult)
                        TT("dve", hi(t4), t4, hi(t1), t1, tb(ST), ST, ALU.mult)
                        TT("dve", lo(t4), t4, lo(t4), t4, hi(t4), t4, ALU.add)
                        if SM:
                            kb.op("act", lambda e: e.copy(out=s5so_v[:, j, :, 0], in_=lo(t2)[:, :, 7]), reads=[t2], writes=[hff])
                            kb.op("act", lambda e: e.copy(out=s5so_v[:, j, :, 1], in_=lo(t4)[:, :, 7]), reads=[t4], writes=[hff])
                        else:
                            kb.op("act", lambda e: e.copy(out=cS5.t[:, j, 0:1], in_=t2.t[:, H - 1:H]), reads=[t2], writes=[cS5])
                            kb.op("act", lambda e: e.copy(out=cS5.t[:, j, 1:2], in_=t4.t[:, H - 1:H]), reads=[t4], writes=[cS5])
                        kb.op("act", lambda e: e.copy(out=hreb.t[:, cs], in_=t2.t[:, 0:H]), reads=[t2], writes=[hreb])
                        kb.op("act", lambda e: e.mul(out=himb.t[:, cs], in_=t4.t[:, 0:H], mul=-1.0), reads=[t4], writes=[himb])
                    kb.op("pe", lambda e: e.matmul(yps.t[:, 0:T], lhsT=cretb.t[:, j, :], rhs=hreb.t[:], start=(jj == 0), stop=False),
                          reads=[cretb, hreb], writes=[yps], inc=True)
                    kb.op("pe", lambda e: e.matmul(yps.t[:, 0:T], lhsT=cimtb.t[:, j, :], rhs=himb.t[:], start=False, stop=(jj == 3)),
                          reads=[cimtb, himb], writes=[yps], inc=True)
                y_ = ntmp()
                STT(y_.t[:], y_, usf.t[:, mo, :], usf, pvc("s5d", mo), yps.t[:, 0:T], yps, ALU.mult, ALU.add, extra=[pv])
                y2 = ntmp()
                TT("dve", y2.t[:], y2, y_.t[:], y_, y_.t[:], y_, ALU.mult)
                TS("dve", y2.t[:], y2, y2.t[:], y2, 0.044715, 1.0, ALU.mult, ALU.add)
                TT("dve", y2.t[:], y2, y2.t[:], y2, y_.t[:], y_, ALU.mult)
                ACT(y2.t[:], y2, y2.t[:], y2, AF.Tanh, scale=0.7978845608028654)
                TS("dve", y2.t[:], y2, y2.t[:], y2, 1.0, 0.5, ALU.add, ALU.mult)
                TT("dve", gS5f.t[:, mo, :], gS5f, y2.t[:], y2, y_.t[:], y_, ALU.mult)
                kb.op("act", lambda e: e.copy(out=gS5b.t[:, mo, :], in_=gS5f.t[:, mo, :]), reads=[gS5f], writes=[gS5b])
            for mo in range(4):
                pz = nps()
                mm_group(pz.t[:, 0:T], pz, [(wglub.t[:, k, mo * 128:(mo + 1) * 128], gS5b.t[:, k, :]) for k in range(4)],
                         [wglub, gS5b])
                sg = ntmp()
                ACT(sg.t[:], sg, pz.t[:, 0:T], pz, AF.Sigmoid, bias=pvc("bglu", mo), extra=[pv])
                TT("dve", ys.t[:, mo, :], ys, gS5f.t[:, mo, :], gS5f, sg.t[:], sg, ALU.mult)
            if STAGE < 4:
                continue
            if not SM:
                ATT = cfg.get("att", "abdf")
                for hp in (range(4) if "a" in ATT else []):
                    rsum = small.t[:, 2 + hp:3 + hp]
                    kb.op("dve", lambda e: e.tensor_reduce(out=rsum, in_=kf.t[:, hp, :], axis=AX.X, op=ALU.add),
                          reads=[kf], writes=[small])
                    kb.op("act", lambda e: e.mul(out=KM.t[:, hp, c:c + 1], in_=rsum, mul=1.0 / 256.0), reads=[small], writes=[KM])
                for hp in (range(4) if "b" in ATT else []):
                    ksq = ntmp()
                    qsq = ntmp()
                    ACT(ksq.t[:], ksq, kf.t[:, hp, :], kf, AF.Square)
                    ACT(qsq.t[:], qsq, qb.t[:, hp, :], qb, AF.Square)
                    for hh in range(2):
                        h = hp * 2 + hh
                        b0 = hh * 64
                        pk = nps()
                        kb.op("pe", lambda e: e.matmul(pk.t[0:1, 0:T], lhsT=ones64[b0:b0 + 64, 0:1], rhs=ksq.t[b0:b0 + 64, :],
                                                       start=True, stop=True), reads=[cst, ksq], writes=[pk])
                        pq = nps()
                        kb.op("pe", lambda e: e.matmul(pq.t[0:1, 0:T], lhsT=ones64[b0:b0 + 64, 0:1], rhs=qsq.t[b0:b0 + 64, :],
                                                       start=True, stop=True), reads=[cst, qsq], writes=[pq])
                        km = small.t[0:1, 8:9]
                        kb.op("dve", lambda e: e.tensor_reduce(out=km, in_=pk.t[0:1, 0:T], axis=AX.X, op=ALU.max),
                              reads=[pk], writes=[small])
                        TT("dve", kmax2.t[0:1, h:h + 1], kmax2, kmax2.t[0:1, h:h + 1], kmax2, km, small, ALU.max)
                        mr = ntmp()
                        TS("dve", mr.t[0:1, :], mr, pq.t[0:1, 0:T], pq, kmax2.t[0:1, h:h + 1], None, ALU.mult, extra=[kmax2])
                        ACT(mr.t[0:1, :], mr, mr.t[0:1, :], mr, AF.Sqrt)
                        TS("dve", MB[h].t[32:33, :], MB[h], mr.t[0:1, :], mr, -1.02, -0.5, ALU.mult, ALU.add)
                if c >= 3:
                    for qs in range(2):
                        Gb = [nps(), nps()]
                        Gvs = [g_.t[:, 0:128].rearrange("p (h n) -> p h n", h=4) for g_ in Gb]
                        for h in range(8):
                            hp, b0 = h // 2, (h % 2) * 64
                            kb.op("pe", lambda e, h=h, hp=hp, b0=b0: e.matmul(
                                Gvs[h % 2][:, h // 2, 0:c], lhsT=qb.t[b0:b0 + 64, hp, qs * 128:(qs + 1) * 128],
                                rhs=KM.t[b0:b0 + 64, hp, 0:c], start=True, stop=True), reads=[qb, KM], writes=[Gb[h % 2]], inc=(h >= 6))
                        if c < 8:
                            kb.op("pool", lambda e: e.memset(gsb.t[:], -1e30), writes=[gsb])
                            for par in range(2):
                                kb.op("dve", lambda e, par=par: e.tensor_copy(
                                    out=gsb.t[:, :, 0:c].rearrange("p (a b) n -> p a b n", b=2)[:, :, par, :], in_=Gvs[par][:, :, 0:c]),
                                    reads=[Gb[par]], writes=[gsb])
                        for h in range(8):
                            gsrc = Gvs[h % 2][:, h // 2, 0:c]
                            src = gsb.t[:, h, :] if c < 8 else gsrc
                            srcb = gsb if c < 8 else Gb[h % 2]
                            kb.op("dve", lambda e, h=h, src=src: e.max(out=top8.t[:, h, :], in_=src), reads=[srcb], writes=[top8])
                            TS("dve", selb[qs].t[:, h, 0:c], selb[qs], gsrc, Gb[h % 2], top8.t[:, h, 2:3], -BIG,
                               ALU.is_lt, ALU.mult, extra=[top8])
                    for h in range(8):
                        tp = nps()
                        for qs in range(2):
                            kb.op("pe", lambda e, qs=qs: e.transpose(out=tp.t[0:32, qs * 128:(qs + 1) * 128], in_=selb[qs].t[:, h, :],
                                                                     identity=ident), reads=[selb[qs], cst], writes=[tp], inc=(qs == 1))
                        kb.op("act", lambda e, h=h, tp=tp: e.copy(out=MB[h].t[0:32, :], in_=tp.t[0:32, 0:T]), reads=[tp], writes=[MB[h]])
                nkt = 2 * c + 2
                ngrp = (nkt + 15) // 16
                for hp in (range(4) if "d" in ATT else []):
                    Ops = [psb[6], psb[7]]
                    for g in range(ngrp):
                        kt0 = g * 16
                        nk_g = min(16, nkt - kt0)
                        kbu = kbuf[kvctr[0] % 2]
                        kvctr[0] += 1
                        kb.dma("sp", kbu.t[:, 0:nk_g * 128], Ks[l].t.ap()[hp * 128:(hp + 1) * 128, kt0 * 128:(kt0 + nk_g) * 128],
                               reads=[Ks[l]], writes=[kbu], owner=kbu)
                        for hh in range(2):
                            h = hp * 2 + hh
                            b0 = hh * 64
                            vbu = vbuf[kvctr[1] % 2]
                            kvctr[1] += 1
                            kb.dma("sp", vbu.t[:, 0:nk_g, :], Vs[l].t.ap()[h, :, kt0:kt0 + nk_g, :], reads=[Vs[l]], writes=[vbu], owner=vbu)
                            for kk in range(nk_g):
                                kt = kt0 + kk
                                n = kt // 2
                                own = kt >= 2 * c
                                S = nps()
                                kb.op("pe", lambda e: e.matmul(S.t[:, 0:T], lhsT=kbu.t[b0:b0 + 64, kk * 128:(kk + 1) * 128],
                                                               rhs=qb.t[b0:b0 + 64, hp, :], start=True, stop=False),
                                      reads=[kbu, qb], writes=[S], inc=False)
                                kb.op("pe", lambda e: e.matmul(S.t[:, 0:T], lhsT=etab.t[:, n * 128:(n + 1) * 128],
                                                               rhs=MB[h].t[:, :], start=False, stop=(not own)),
                                      reads=[etab, MB[h]], writes=[S], inc=(not own))
                                if own:
                                    kb.op("pe", lambda e: e.matmul(S.t[:, 0:T], lhsT=identb.t[:], rhs=causb.t[:, kt - 2 * c, :],
                                                                   start=False, stop=True),
                                          reads=[identb, causb], writes=[S], inc=True)
                                P_ = Pt[kvctr[2] % 2]
                                kvctr[2] += 1
                                ACT(P_.t[:], P_, S.t[:, 0:T], S, AF.Exp)
                                kb.op("pe", lambda e: e.matmul(Ops[hh].t[0:65, 0:T], lhsT=vbu.t[:, kk, :], rhs=P_.t[:],
                                                               start=(kt == 0), stop=(kt == nkt - 1)),
                                      reads=[vbu, P_], writes=[Ops[hh]], inc=True)
                    for hh in (range(2) if "f" in ATT else []):
                        b0 = hh * 64
                        kb.op("act", lambda e: e.copy(out=Osb.t[:], in_=Ops[hh].t[0:65, 0:T]), reads=[Ops[hh]], writes=[Osb])
                        rden = ntmp()
                        kb.op("dve", lambda e: e.reciprocal(out=rden.t[64:65, :], in_=Osb.t[64:65, :]), reads=[Osb], writes=[rden])
                        bc = nps()
                        kb.op("pe", lambda e: e.matmul(bc.t[0:64, 0:T], lhsT=ones64[64:65, 0:64], rhs=rden.t[64:65, :],
                                                       start=True, stop=True), reads=[cst, rden], writes=[bc])
                        TT("dve", yc.t[b0:b0 + 64, hp, :], yc, Osb.t[0:64, :], Osb, bc.t[0:64, 0:T], bc, ALU.mult)

            else:
                sample_attention(l)
                cvv = vtok[1].t[:, 0:384].rearrange("p (m s j) -> p m s j", m=8, s=16)
                kb.op("dve", lambda e: e.tensor_copy(out=cvv, in_=xe_v[:, :, :, 8:11]), reads=[hff], writes=[vtok[1]])
                kb.dma("pool", convs_o.t.ap()[l], cvv, reads=[vtok[1]], writes=[convs_o], owner=vtok[1])
                kb.dma("pool", rglrus_o.t.ap()[l], rgso_v, reads=[hff], writes=[rglrus_o], owner=hff)
                kb.dma("pool", s5s_o.t.ap()[l], s5so_v, reads=[hff], writes=[s5s_o], owner=hff)
            if STAGE < 5:
                continue
            branches = [(wb_bra, ya, 8), (wb_brs, ys, 4), (wb_brc, yc, 4)]
            for half in range(2):
                for i, (wbr, ybr, nk) in enumerate(branches):
                    wg = load_w(wb_in, l, 0, 8, 3072 + i * 1024 + half * 512, 512)
                    wr = load_w(wbr, l, 0, nk, half * 512, 512)
                    for mi in range(4):
                        pg = nps()
                        mm_group(pg.t[:, 0:T], pg, [(wg.t[:, kt, mi * 128:(mi + 1) * 128], xb.t[:, kt, :]) for kt in range(8)],
                                 [wg, xb])
                        sg = ntmp()
                        ACT(sg.t[:], sg, pg.t[:, 0:T], pg, AF.Sigmoid)
                        pbr = nps()
                        mm_group(pbr.t[:, 0:T], pbr, [(wr.t[:, kt, mi * 128:(mi + 1) * 128], ybr.t[:, kt, :]) for kt in range(nk)],
                                 [wr, ybr])
                        if i == 0:
                            TT("dve", gS5f.t[:, mi, :], gS5f, sg.t[:], sg, pbr.t[:, 0:T], pbr, ALU.mult)
                        else:
                            TT("dve", sg.t[:], sg, sg.t[:], sg, pbr.t[:, 0:T], pbr, ALU.mult)
                            TT("dve", gS5f.t[:, mi, :], gS5f, gS5f.t[:, mi, :], gS5f, sg.t[:], sg, ALU.add)
                for mi in range(4):
                    m = half * 4 + mi
                    kb.op("act", lambda e, m=m, mi=mi: e.copy(out=mg.t[:, m, :], in_=gS5f.t[:, mi, :]), reads=[gS5f], writes=[mg])
            for half in range(2):
                wo = load_w(wb_out, l, 0, 8, half * 512, 512)
                for mi in range(4):
                    m = half * 4 + mi
                    po = nps()
                    mm_group(po.t[:, 0:T], po, [(wo.t[:, kt, mi * 128:(mi + 1) * 128], mg.t[:, kt, :]) for kt in range(8)],
                             [wo, mg])
                    STT(xf.t[:, m, :], xf, xf.t[:, m, :], xf, ALPHA, po.t[:, 0:T], po, ALU.mult, ALU.add)
            layer_norm("g1", "b1")
            for s in range(11):
                wf = nws()
                load_w(wb_f1, l, 0, 8, s * 256, 256, slot=wf, soff=0)
                load_w(wb_f1, l, 0, 8, DFF + s * 256, 256, slot=wf, soff=256)
                for mi in range(2):
                    pg = nps()
                    mm_group(pg.t[:, 0:T], pg, [(wf.t[:, kt, mi * 128:(mi + 1) * 128], xb.t[:, kt, :]) for kt in range(8)], [wf, xb])
                    pu = nps()
                    mm_group(pu.t[:, 0:T], pu, [(wf.t[:, kt, 256 + mi * 128:256 + (mi + 1) * 128], xb.t[:, kt, :]) for kt in range(8)],
                             [wf, xb])
                    sg = ntmp()
                    ACT(sg.t[:], sg, pg.t[:, 0:T], pg, AF.Silu)
                    TT("dve", hff.t[:, 2 * s + mi, :], hff, sg.t[:], sg, pu.t[:, 0:T], pu, ALU.mult)
            for half in range(2):
                pacc = [nps() for _ in range(4)]
                for ks, nk in [(0, 8), (8, 8), (16, 6)]:
                    w2 = load_w(wb_f2, l, ks * 128, nk, half * 512, 512)
                    for mi in range(4):
                        mm_group(pacc[mi].t[:, 0:T], pacc[mi],
                                 [(w2.t[:, kt, mi * 128:(mi + 1) * 128], hff.t[:, ks + kt, :]) for kt in range(nk)],
                                 [w2, hff], first=(ks == 0), last=(ks == 16))
                for mi in range(4):
                    m = half * 4 + mi
                    STT(xf.t[:, m, :], xf, xf.t[:, m, :], xf, ALPHA, pacc[mi].t[:, 0:T], pacc[mi], ALU.mult, ALU.add)
            layer_norm("g2", "b2")
            if SM:
                dst = ysT if l == NL - 1 else x1sT
                kb.dma("pool", dst.t.ap().rearrange("(m p) t -> p m t", p=128), xf.t[:, :, 0:128],
                       reads=[xf], writes=[dst], owner=xf)
            else:
                dst = yT if l == NL - 1 else x1T
                kb.dma("pool", dst.t.ap()[:, t0:t0 + T].rearrange("(m p) t -> p m t", p=128), xf.t[:],
                       reads=[xf], writes=[dst], owner=xf)
        if STAGE >= 2 and not SAMPLE:
            kb.dma("pool", conv_o.t.ap()[l], xr.t[:, :, T:T + 3], reads=[xr], writes=[conv_o], owner=xr)
            kb.dma("pool", rglru_o.t.ap()[l], carryA.t[:], reads=[carryA], writes=[rglru_o], owner=carryA)
        if STAGE >= 3 and not SAMPLE:
            kb.dma("pool", s5_o.t.ap()[l], cS5.t[:], reads=[cS5], writes=[s5_o], owner=cS5)

    outs = [yT, newkT, newv, conv_o, rglru_o, s5_o]
    if SAMPLE:
        outs += [ysT, newkTs, newvs, convs_o, rglrus_o, s5s_o, x1sT]
    kb.wait_all("sp", outs)
    kb.wait_all("sp", Ks + Vs + [x1T])
    kb.declared = declared
    return nc, es, kb


def _consts():
    cst = np.zeros((128, NCST), np.float32)
    cst[:, 0:128] = np.eye(128, dtype=np.float32)
    R = np.zeros((128, 128), np.float32)
    for m in range(128):
        if (m % 64) < 32:
            R[m + 32, m] = -1.0
        else:
            R[m - 32, m] = 1.0
    cst[:, 128:256] = R
    kk = np.arange(128)[:, None]
    qi = np.arange(T)[None, :]
    for a in range(2):
        cst[:, 256 + a * T:256 + (a + 1) * T] = np.where(128 * a + kk <= qi, 0.0, -BIG)
    cst[:, 768:896] = np.arange(1, 129, dtype=np.float32)[None, :]
    cst[:, 896:960] = 1.0
    cst[:, 960:1088] = 1.0 / 1024.0
    cst[:, 1088] = np.arange(128, dtype=np.float32)
    etab = np.zeros((128, 32, 128), np.float32)
    for n in range(32):
        etab[n, n, :] = 1.0
    etab[32, :, :] = 1.0
    return cst, etab.reshape(128, 32 * 128)


def _rope_tables(SEQ=SEQ):
    half = 32
    inv = np.power(np.float32(10000.0), -np.arange(half, dtype=np.float32) * np.float32(2.0 / 64)).astype(np.float32)
    pos = np.arange(SEQ, dtype=np.float32)
    ang = (pos[None, :] * inv[:, None]).astype(np.float32)
    c = np.cos(ang).astype(np.float32)
    s = np.sin(ang).astype(np.float32)
    ropec = np.tile(c, (4, 1))
    ropes = np.tile(s, (4, 1))
    return np.ascontiguousarray(ropec), np.ascontiguousarray(ropes)


def _pm(v, ntile):
    L = v.shape[0]
    return np.ascontiguousarray(v.reshape(L, ntile, 128).transpose(0, 2, 1))


def _prep_shared(inp, SEQ=SEQ):
    f = np.float32
    sh = {}
    for k in ["w_in", "w_br_rnn", "w_br_ssm", "w_br_attn", "w_out", "w_ffn_in", "w_ffn_out", "s5_w_glu"]:
        sh[k] = np.ascontiguousarray(inp[k], dtype=f)
    for nm, src in [("rgwa", "rg_w_a"), ("rgwx", "rg_w_x")]:
        w = np.asarray(inp[src], f)
        o = np.zeros((DEPTH, 8, 128, 128), f)
        for j in range(8):
            o[:, j, 0:64, 0:64] = w[:, 2 * j]
            o[:, j, 64:128, 64:128] = w[:, 2 * j + 1]
        sh[nm] = o
    bre = np.asarray(inp["s5_b_re"], f)
    bim = np.asarray(inp["s5_b_im"], f)
    cre = np.asarray(inp["s5_c_re"], f)
    cim = np.asarray(inp["s5_c_im"], f)
    bret = np.zeros((DEPTH, 16, 128, 128), f)
    bimt = np.zeros((DEPTH, 16, 128, 128), f)
    cret = np.zeros((DEPTH, 16, 128, 128), f)
    cimt = np.zeros((DEPTH, 16, 128, 128), f)
    for j in range(16):
        for gl in range(2):
            g = 2 * j + gl
            r0 = 16 * (g % 8)
            bret[:, j, r0:r0 + 16, gl * 64:(gl + 1) * 64] = bre[:, g].transpose(0, 2, 1)
            bimt[:, j, r0:r0 + 16, gl * 64:(gl + 1) * 64] = bim[:, g].transpose(0, 2, 1)
            cret[:, j, gl * 64:(gl + 1) * 64, r0:r0 + 16] = cre[:, g].transpose(0, 2, 1)
            cimt[:, j, gl * 64:(gl + 1) * 64, r0:r0 + 16] = cim[:, g].transpose(0, 2, 1)
    sh["bret"], sh["bimt"], sh["cret"], sh["cimt"] = bret, bimt, cret, cimt
    pv = np.zeros((DEPTH, 128, NPV), f)
    cw = np.asarray(inp["conv_w"], f)
    for j in range(4):
        pv[:, :, PV["cw%d" % j]:PV["cw%d" % j] + 8] = _pm(cw[:, j], 8)
    for nm, src, nt in [("cb", "conv_b", 8), ("ba", "rg_b_a", 8), ("bx", "rg_b_x", 8), ("lam", "rg_lambda", 8),
                        ("s5d", "s5_d", 4), ("bglu", "s5_b_glu", 4), ("g1", "ln1_g", 8), ("b1", "ln1_b", 8),
                        ("g2", "ln2_g", 8), ("b2", "ln2_b", 8)]:
        pv[:, :, PV[nm]:PV[nm] + nt] = _pm(np.asarray(inp[src], f), nt)
    sh["pv"] = pv
    ps5 = np.zeros((DEPTH, 128, 48), f)
    lre = np.asarray(inp["s5_lambda_re"], f)
    lim = np.asarray(inp["s5_lambda_im"], f)
    lst = np.asarray(inp["s5_log_step"], f)
    for j in range(16):
        for gl in range(2):
            g = 2 * j + gl
            ps5[:, gl * 64:(gl + 1) * 64, j] = lre[:, g]
            ps5[:, gl * 64:(gl + 1) * 64, 16 + j] = lim[:, g]
            ps5[:, gl * 64:(gl + 1) * 64, 32 + j] = lst[:, g][:, None]
    sh["ps5"] = ps5
    sh["ropec"], sh["ropes"] = _rope_tables(SEQ)
    sh["cst"], sh["etab"] = _consts()
    return sh


_CFG = {}


def _prep_sample(inp, c, npool=2560, page_table=None):
    f = np.float32
    m = {}
    b0 = 16 * c
    xs = np.asarray(inp["x_sample"], f)[b0:b0 + 16]
    m["xsT"] = np.ascontiguousarray(xs.reshape(128, D).T)
    half = 32
    inv = np.power(np.float32(10000.0), -np.arange(half, dtype=np.float32) * np.float32(2.0 / 64)).astype(np.float32)
    pos = (2048 + np.arange(8)).astype(np.float32)
    ang = (pos[None, :] * inv[:, None]).astype(np.float32)
    cc = np.zeros((128, T), f)
    ss = np.zeros((128, T), f)
    cc[:, 0:128] = np.tile(np.tile(np.cos(ang).astype(f), (4, 1)), (1, 16))
    ss[:, 0:128] = np.tile(np.tile(np.sin(ang).astype(f), (4, 1)), (1, 16))
    m["ropecS"], m["ropesS"] = cc, ss
    sc = np.asarray(inp["state_conv"], f)[:, b0:b0 + 16]
    m["convS"] = np.ascontiguousarray(sc.reshape(DEPTH, 16, 3, 8, 128).transpose(0, 4, 3, 1, 2))
    sr = np.asarray(inp["state_rglru"], f)[:, b0:b0 + 16]
    m["rgS"] = np.ascontiguousarray(sr.reshape(DEPTH, 16, 8, 128).transpose(0, 3, 2, 1))
    s5 = np.stack([np.asarray(inp["state_s5_re"], f)[:, b0:b0 + 16], np.asarray(inp["state_s5_im"], f)[:, b0:b0 + 16]], -1)
    s5 = s5.reshape(DEPTH, 16, 16, 2, 64, 2).transpose(0, 3, 4, 2, 1, 5).reshape(DEPTH, 128, 16, 16, 2)
    m["s5S"] = np.ascontiguousarray(s5)
    pt = np.asarray(inp["page_table"] if page_table is None else page_table, np.int32)[b0:b0 + 16]
    m["ptab"] = np.ascontiguousarray(np.broadcast_to(pt.reshape(1, 256), (128, 256))).astype(np.int32)
    kk = np.arange(128)
    same = (kk[:, None] // 8) == (kk[None, :] // 8)
    caus = (kk[:, None] % 8) <= (kk[None, :] % 8)
    m["cmaskS"] = np.where(same & caus, 0.0, -BIG).astype(f)
    return m


def run(inp, cfg):
    nc, es, kbld = build(cfg)
    sq = cfg.get("seq", SEQ)
    sh = _prep_shared(inp, sq)
    xp = np.asarray(inp["x_prompt"], np.float32)[:, :sq]
    in_maps = []
    ncores = cfg.get("ncores", 8)
    if cfg.get("sample", True):
        npool = cfg.get("npool", 2560)
        ckf = np.asarray(inp["cache_k"], np.float32).reshape(DEPTH * npool * 128, 512)
        cvf = np.asarray(inp["cache_v"], np.float32).reshape(DEPTH * npool * 128, 512)
    for c in range(ncores):
        m = dict(sh)
        m["xT"] = np.ascontiguousarray(xp[c // 4].T)
        if cfg.get("sample", True):
            m.update(_prep_sample(inp, c))
            m["cache_k"] = ckf
            m["cache_v"] = cvf
        in_maps.append({k: v for k, v in m.items() if k in kbld.declared})
    with es:
        res = run_bass_kernel_spmd(nc, in_maps, core_ids=list(range(ncores)))
    return res.results


def kernel(**inputs):
    r = run(inputs, dict(_CFG))
    f = np.float32
    B, S, NS = 2, SEQ, 128
    yp = np.zeros((B, S, D), f)
    kp = np.zeros((DEPTH, B, S, 8, 64), f)
    vp = np.zeros((DEPTH, B, S, 8, 64), f)
    cp = np.zeros((DEPTH, B, 3, D), f)
    hp = np.zeros((DEPTH, B, D), f)
    s5rp = np.zeros((DEPTH, B, 32, 64), f)
    s5ip = np.zeros((DEPTH, B, 32, 64), f)
    for b in range(B):
        o = r[4 * b]
        yp[b] = o["yT"].T
        for l in range(DEPTH):
            kp[l, b] = o["newkT"][l].T.reshape(S, 8, 64)
            vp[l, b] = o["newv"][l].reshape(S, 8, 64)
            cp[l, b] = o["conv_o"][l].transpose(2, 1, 0).reshape(3, D)
            hp[l, b] = o["rglru_o"][l].T.reshape(D)
            s5 = o["s5_o"][l].reshape(2, 64, 16, 2).transpose(2, 0, 1, 3).reshape(32, 64, 2)
            s5rp[l, b] = s5[:, :, 0]
            s5ip[l, b] = s5[:, :, 1]
    ys = np.zeros((NS, 8, D), f)
    ks = np.zeros((DEPTH, NS, 8, 8, 64), f)
    vs = np.zeros((DEPTH, NS, 8, 8, 64), f)
    cs = np.zeros((DEPTH, NS, 3, D), f)
    hs = np.zeros((DEPTH, NS, D), f)
    s5rs = np.zeros((DEPTH, NS, 32, 64), f)
    s5is = np.zeros((DEPTH, NS, 32, 64), f)
    for c in range(8):
        o = r[c]
        b0 = 16 * c
        ys[b0:b0 + 16] = o["ysT"].T.reshape(16, 8, D)
        for l in range(DEPTH):
            ks[l, b0:b0 + 16] = o["newkTs"][l].T.reshape(16, 8, 8, 64)
            vs[l, b0:b0 + 16] = o["newvs"][l].reshape(16, 8, 8, 64)
            cs[l, b0:b0 + 16] = o["convs_o"][l].transpose(2, 3, 1, 0).reshape(16, 3, D)
            hs[l, b0:b0 + 16] = o["rglrus_o"][l].transpose(2, 1, 0).reshape(16, D)
            s5 = o["s5s_o"][l].reshape(2, 64, 16, 16, 2).transpose(3, 2, 0, 1, 4).reshape(16, 32, 64, 2)
            s5rs[l, b0:b0 + 16] = s5[..., 0]
            s5is[l, b0:b0 + 16] = s5[..., 1]
    return (yp, ys, kp, vp, cp, hp, s5rp, s5ip, ks, vs, cs, hs, s5rs, s5is)
```

```python
import contextlib
import numpy as np
import concourse.bass as bass
import concourse.mybir as mybir
from concourse.bass_utils import run_bass_kernel_spmd

F32 = mybir.dt.float32
BF16 = mybir.dt.bfloat16
I32 = mybir.dt.int32
ALU = mybir.AluOpType
AF = mybir.ActivationFunctionType
AX = mybir.AxisListType

D = 1024
SEQ = 8192
DEPTH = 2
NIN = 6144
DFF = 2816
T = 256
BIG = 30000.0
ALPHA = (2.0 * DEPTH) ** 0.25
LN_EPS = 1e-5
NCST = 1152

PV = {}
_o = 0
for _n, _c in [("cw0", 8), ("cw1", 8), ("cw2", 8), ("cw3", 8), ("cb", 8), ("ba", 8), ("bx", 8), ("lam", 8),
               ("s5d", 4), ("bglu", 4), ("g1", 8), ("b1", 8), ("g2", 8), ("b2", 8)]:
    PV[_n] = _o
    _o += _c
NPV = _o


class Buf:
    __slots__ = ("name", "t", "lw", "rd", "dsem", "dcnt", "excl")

    def __init__(self, name, t):
        self.name = name
        self.t = t
        self.lw = None
        self.rd = {}
        self.dsem = None
        self.dcnt = 0
        self.excl = False


class KB:
    def __init__(self, nc, es):
        self.nc = nc
        self.es = es
        self.E = {"pe": nc.tensor, "act": nc.scalar, "dve": nc.vector, "pool": nc.gpsimd, "sp": nc.sync}
        self.sem = {k: es.enter_context(nc.semaphore("s_" + k)) for k in ["pe", "act", "dve", "pool"]}
        self.cnt = {k: 0 for k in self.sem}
        self.waited = {k: {} for k in self.E}
        self.nbuf = 0
        self.semlatest = {}
        self.ninst = 0

    def sb(self, name, shape, dt):
        t = self.es.enter_context(self.nc.sbuf_tensor("sb_" + name, list(shape), dt))
        return Buf(name, t)

    def ps(self, name, shape, dt=F32):
        t = self.es.enter_context(self.nc.psum_tensor("ps_" + name, list(shape), dt))
        b = Buf(name, t)
        b.excl = True
        return b

    def dram(self, name, shape, dt, kind="Internal"):
        t = self.nc.dram_tensor(name, list(shape), dt, kind=kind)
        b = Buf(name, t)
        return b

    def _wait(self, eng, tickets):
        w = self.waited[eng]
        for (key, h, val) in tickets:
            if key in self.semlatest:
                val = self.semlatest[key]
            if w.get(key, 0) >= val:
                continue
            self.E[eng].wait_ge(h, val)
            w[key] = val

    def _deps(self, eng, reads, writes):
        tk = []
        for b in reads:
            if b.lw is not None:
                tk.append(b.lw)
            if b.excl:
                tk.extend(t for t in b.rd.values() if t[0] != eng)
        for b in writes:
            if b.lw is not None:
                tk.append(b.lw)
            tk.extend(b.rd.values())
        if eng == "pe":
            tk = [t for t in tk if t[0] != "pe"]
        return tk

    def _mark(self, tk, reads, writes):
        for b in reads:
            old = b.rd.get(tk[0])
            if old is None or old[2] < tk[2]:
                b.rd[tk[0]] = tk
        for b in writes:
            b.lw = tk
            b.rd = {}

    def op(self, eng, fn, reads=(), writes=(), inc=True):
        self._wait(eng, self._deps(eng, reads, writes))
        ins = fn(self.E[eng])
        self.ninst += 1
        if inc:
            self.cnt[eng] += 1
            ins.then_inc(self.sem[eng], 1)
            tk = (eng, self.sem[eng], self.cnt[eng])
        else:
            tk = (eng, self.sem[eng], self.cnt[eng] + 1)
        self._mark(tk, reads, writes)
        return tk

    def dma(self, q, out, in_, reads=(), writes=(), owner=None, **kw):
        self._wait(q, self._deps(q, reads, writes))
        b = owner
        if b.dsem is None:
            b.dsem = {}
            b.dcnt = {}
        if q not in b.dsem:
            b.dsem[q] = self.es.enter_context(self.nc.semaphore("d_%s_%s" % (b.name, q)))
            b.dcnt[q] = 0
        b.dcnt[q] += 16
        self.E[q].dma_start(out=out, in_=in_, **kw).then_inc(b.dsem[q], 16)
        self.ninst += 1
        tk = ("d_%s_%s" % (b.name, q), b.dsem[q], b.dcnt[q])
        self.semlatest[tk[0]] = tk[2]
        self._mark(tk, reads, writes)
        return tk

    def dma_ind(self, owner, out, in_, idx_ap, reads=(), writes=None):
        q = "pool"
        writes = [owner] if writes is None else writes
        self._wait(q, self._deps(q, reads, writes))
        b = owner
        if b.dsem is None:
            b.dsem = {}
            b.dcnt = {}
        key = "ind"
        if key not in b.dsem:
            b.dsem[key] = self.es.enter_context(self.nc.semaphore("d_%s_ind" % b.name))
            b.dcnt[key] = 0
        b.dcnt[key] += 16
        self.nc.gpsimd.indirect_dma_start(out=out, out_offset=None, in_=in_,
                                          in_offset=bass.IndirectOffsetOnAxis(ap=idx_ap, axis=0)).then_inc(b.dsem[key], 16)
        self.ninst += 1
        tk = ("d_%s_ind" % b.name, b.dsem[key], b.dcnt[key])
        self.semlatest[tk[0]] = tk[2]
        self._mark(tk, reads, writes)
        return tk

    def wait_all(self, eng, bufs):
        tk = []
        for b in bufs:
            if b.lw is not None:
                tk.append(b.lw)
            tk.extend(b.rd.values())
        self._wait(eng, tk)


def build(cfg):
    SEQ = cfg.get("seq", 8192)
    NCH = cfg.get("nch", SEQ // T)
    NL = cfg.get("nl", DEPTH)
    STAGE = cfg.get("stage", 99)
    nc = bass.Bass("TRN2", target_bir_lowering=False)
    es = contextlib.ExitStack()
    kb = KB(nc, es)
    declared = []

    def din(name, shape, dt=F32):
        declared.append(name)
        return kb.dram(name, shape, dt, kind="ExternalInput")

    def dout(name, shape, dt=F32):
        return kb.dram(name, shape, dt, kind="ExternalOutput")

    xT = din("xT", [D, SEQ])
    w_in = din("w_in", [DEPTH, D, NIN])
    w_bra = din("w_br_rnn", [DEPTH, D, D])
    w_brs = din("w_br_ssm", [DEPTH, 512, D])
    w_brc = din("w_br_attn", [DEPTH, 512, D])
    w_out = din("w_out", [DEPTH, D, D])
    w_f1 = din("w_ffn_in", [DEPTH, D, 2 * DFF])
    w_f2 = din("w_ffn_out", [DEPTH, DFF, D])
    w_glu = din("s5_w_glu", [DEPTH, 512, 512])
    rgwa = din("rgwa", [DEPTH, 8, 128, 128])
    rgwx = din("rgwx", [DEPTH, 8, 128, 128])
    bret = din("bret", [DEPTH, 16, 128, 128])
    bimt = din("bimt", [DEPTH, 16, 128, 128])
    cret = din("cret", [DEPTH, 16, 128, 128])
    cimt = din("cimt", [DEPTH, 16, 128, 128])
    pv_d = din("pv", [DEPTH, 128, NPV])
    ps5_d = din("ps5", [DEPTH, 128, 48])
    ropec_d = din("ropec", [128, SEQ])
    ropes_d = din("ropes", [128, SEQ])
    cst_d = din("cst", [128, NCST])
    etab_d = din("etab", [128, 32 * 128])

    SAMPLE = cfg.get("sample", True)
    if SAMPLE:
        xsT = din("xsT", [D, 128])
        ropecS_d = din("ropecS", [128, T])
        ropesS_d = din("ropesS", [128, T])
        convS_d = din("convS", [DEPTH, 128, 8, 16, 3])
        rgS_d = din("rgS", [DEPTH, 128, 8, 16])
        s5S_d = din("s5S", [DEPTH, 128, 16, 16, 2])
        ptab_d = din("ptab", [128, 256], I32)
        NPOOL = cfg.get("npool", 2560)
        cacheK = din("cache_k", [DEPTH * NPOOL * 128, 512])
        cacheV = din("cache_v", [DEPTH * NPOOL * 128, 512])
        cmaskS_d = din("cmaskS", [128, 128])
        ysT = dout("ysT", [D, 128])
        newkTs = dout("newkTs", [DEPTH, 512, 128])
        newvs = dout("newvs", [DEPTH, 128, 512])
        convs_o = dout("convs_o", [DEPTH, 128, 8, 16, 3])
        rglrus_o = dout("rglrus_o", [DEPTH, 128, 8, 16])
        s5s_o = dout("s5s_o", [DEPTH, 128, 16, 16, 2])
        x1sT = kb.dram("x1sT", [D, 128], F32)
    yT = dout("yT", [D, SEQ])
    newkT = dout("newkT", [DEPTH, 512, SEQ])
    newv = dout("newv", [DEPTH, SEQ, 512])
    conv_o = dout("conv_o", [DEPTH, 128, 8, 3])
    rglru_o = dout("rglru_o", [DEPTH, 128, 8])
    s5_o = dout("s5_o", [DEPTH, 128, 16, 2])

    wb_in = kb.dram("wb_in", [DEPTH, D, NIN], BF16)
    wb_bra = kb.dram("wb_bra", [DEPTH, D, D], BF16)
    wb_brs = kb.dram("wb_brs", [DEPTH, 512, D], BF16)
    wb_brc = kb.dram("wb_brc", [DEPTH, 512, D], BF16)
    wb_out = kb.dram("wb_out", [DEPTH, D, D], BF16)
    wb_f1 = kb.dram("wb_f1", [DEPTH, D, 2 * DFF], BF16)
    wb_f2 = kb.dram("wb_f2", [DEPTH, DFF, D], BF16)
    Ks = [kb.dram("Ks%d" % l, [512, SEQ], BF16) for l in range(DEPTH)]
    Vs = [kb.dram("Vs%d" % l, [8, 128, SEQ // 128, 65], BF16) for l in range(DEPTH)]
    x1T = kb.dram("x1T", [D, SEQ], F32)

    es2 = contextlib.ExitStack()
    stg = []
    for i in range(2):
        t_ = es2.enter_context(nc.sbuf_tensor("sb_stg%d" % i, [128, 12288], BF16))
        stg.append(Buf("stg%d" % i, t_))
    sctr = [0]

    def cast_w(src, dst, rows, cols, l):
        nk_all = rows // 128
        per = max(1, 12288 // cols)
        k = 0
        while k < nk_all:
            nk = min(per, nk_all - k)
            sb_ = stg[sctr[0] % 2]
            sctr[0] += 1
            view = sb_.t[:, 0:nk * cols].rearrange("p (kt n) -> p kt n", kt=nk)
            kb.dma("pool", view, src.t.ap()[l, k * 128:(k + nk) * 128, :].rearrange("(kt p) n -> p kt n", p=128),
                   reads=[src], writes=[sb_], owner=sb_)
            kb.dma("sp", dst.t.ap()[l, k * 128:(k + nk) * 128, :].rearrange("(kt p) n -> p kt n", p=128), view,
                   reads=[sb_], writes=[dst], owner=sb_)
            k += nk

    for l in range(NL):
        cast_w(w_in, wb_in, D, NIN, l)
        if STAGE >= 2:
            cast_w(w_bra, wb_bra, D, D, l)
            cast_w(w_brs, wb_brs, 512, D, l)
            cast_w(w_brc, wb_brc, 512, D, l)
            cast_w(w_out, wb_out, D, D, l)
            cast_w(w_f1, wb_f1, D, 2 * DFF, l)
            cast_w(w_f2, wb_f2, DFF, D, l)
    for e_ in ["pe", "act", "dve", "pool", "sp"]:
        kb.wait_all(e_, stg)
    es2.close()

    cst = kb.sb("cst", [128, NCST], F32)
    identb = kb.sb("identb", [128, 128], BF16)
    causb = kb.sb("causb", [128, 2, T], BF16)
    etab = kb.sb("etab", [128, 32 * 128], BF16)
    pv = kb.sb("pv", [128, NPV], F32)
    ps5 = kb.sb("ps5", [128, 48], F32)
    xf = kb.sb("xf", [128, 8, T], F32)
    xb = kb.sb("xb", [128, 8, T], BF16)
    wsl = [kb.sb("wsl%d" % i, [128, 8, 512], BF16) for i in range(3)]
    wctr = [0]
    ropec = kb.sb("ropec", [128, T], F32)
    ropes = kb.sb("ropes", [128, T], F32)
    xr = kb.sb("xr", [128, 8, 3 + T], F32)
    usf = kb.sb("usf", [128, 4, T], F32)
    usb = kb.sb("usb", [128, 4, T], BF16)
    qb = kb.sb("qb", [128, 4, T], BF16)
    kf = kb.sb("kf", [128, 4, T], F32)
    kbf = kb.sb("kbf", [128, 4, T], BF16)
    vtok = [kb.sb("vtok%d" % i, [128, 512], F32) for i in range(2)]
    vaug = [kb.sb("vaug%d" % i, [128, 8, 65], BF16) for i in range(2)]
    NTMP = 14
    tmp = [kb.sb("tmp%d" % i, [128, T], F32) for i in range(NTMP)]
    tctr = [0]
    psb = [kb.ps("psb%d" % i, [128, 512], F32) for i in range(8)]
    psctr = [0]
    rgwab = kb.sb("rgwab", [128, 8, 128], BF16)
    rgwxb = kb.sb("rgwxb", [128, 8, 128], BF16)
    bretb = kb.sb("bretb", [128, 16, 128], BF16)
    bimtb = kb.sb("bimtb", [128, 16, 128], BF16)
    cretb = kb.sb("cretb", [128, 16, 128], BF16)
    cimtb = kb.sb("cimtb", [128, 16, 128], BF16)
    wglub = kb.sb("wglub", [128, 4, 512], BF16)
    clam = kb.sb("clam", [128, 8], F32)
    carryA = kb.sb("carryA", [128, 8], F32)
    cS5 = kb.sb("cS5", [128, 16, 2], F32)
    s5p = kb.sb("s5p", [128, 8, 16], F32)
    CT = kb.sb("CT", [128, 16, 128], F32)
    ST = kb.sb("ST", [128, 16, 128], F32)
    MRE = kb.sb("MRE", [128, 16, 128], F32)
    MIM = kb.sb("MIM", [128, 16, 128], F32)
    ya = kb.sb("ya", [128, 8, T], BF16)
    ys = kb.sb("ys", [128, 4, T], BF16)
    yc = kb.sb("yc", [128, 4, T], BF16)
    gS5f = kb.sb("gS5f", [128, 4, T], F32)
    gS5b = kb.sb("gS5b", [128, 4, T], BF16)
    hrebs = [kb.sb("hreb%d" % i, [128, T], BF16) for i in range(2)]
    himbs = [kb.sb("himb%d" % i, [128, T], BF16) for i in range(2)]
    mg = kb.sb("mg", [128, 8, T], BF16)
    hff = kb.sb("hff", [128, 22, T], BF16)
    KM = kb.sb("KM", [128, 4, 32], BF16)
    kmax2 = kb.sb("kmax2", [1, 8], F32)
    MB = [kb.sb("MB%d" % h, [128, T], BF16) for h in range(8)]
    selb = [kb.sb("selb%d" % i, [128, 8, 32], F32) for i in range(2)]
    gsb = kb.sb("gsb", [128, 8, 8], F32)
    top8 = kb.sb("top8", [128, 8, 8], F32)
    kbuf = [kb.sb("kbuf%d" % i, [128, 2048], BF16) for i in range(2)]
    vbuf = [kb.sb("vbuf%d" % i, [128, 16, 65], BF16) for i in range(2)]
    Pt = [kb.sb("Pt%d" % i, [128, T], BF16) for i in range(2)]
    Osb = kb.sb("Osb", [65, T], F32)
    small = kb.sb("small", [128, 16], F32)
    kvctr = [0, 0, 0]
    if SAMPLE:
        hff32 = hff.t[:].rearrange("p a b -> p (a b)").bitcast(F32)
        xe_v = hff32[:, 0:1408].rearrange("p (m s j) -> p m s j", m=8, s=16)
        s5s0_v = hff32[:, 1408:1920].rearrange("p (j s r) -> p j s r", j=16, s=16)
        s5so_v = hff32[:, 1920:2432].rearrange("p (j s r) -> p j s r", j=16, s=16)
        h0s_v = hff32[:, 2432:2560].rearrange("p (m s) -> p m s", m=8)
        rgso_v = hff32[:, 2560:2688].rearrange("p (m s) -> p m s", m=8)
        idx = kb.sb("idx", [128, 256], I32)
        cmaskS = kb.sb("cmaskS", [128, 128], BF16)
        MBs = kb.sb("MBs", [128, 64], BF16)
        KMs = kb.sb("KMs", [128, 4, 8], BF16)

    ident = cst.t[:, 0:128]
    rrot = cst.t[:, 128:256]
    tau = cst.t[:, 768:896]
    ones64 = cst.t[:, 896:960]
    onesD = cst.t[:, 960:1088]

    def nps():
        b = psb[psctr[0] % 6]
        psctr[0] += 1
        return b

    def nws():
        b = wsl[wctr[0] % 3]
        wctr[0] += 1
        return b

    def ntmp():
        b = tmp[tctr[0] % NTMP]
        tctr[0] += 1
        return b

    def pvc(name, m):
        return pv.t[:, PV[name] + m:PV[name] + m + 1]

    kb.dma("sp", cst.t[:], cst_d.t.ap(), reads=[cst_d], writes=[cst], owner=cst)
    kb.op("dve", lambda e: e.tensor_copy(out=identb.t[:], in_=cst.t[:, 0:128]), reads=[cst], writes=[identb])
    kb.op("dve", lambda e: e.tensor_copy(out=causb.t[:], in_=cst.t[:, 256:768].rearrange("p (a t) -> p a t", a=2)),
          reads=[cst], writes=[causb])
    kb.dma("pool", etab.t[:], etab_d.t.ap(), reads=[etab_d], writes=[etab], owner=etab)
    for i in range(2):
        kb.op("pool", lambda e, i=i: e.memset(vaug[i].t[:], 1.0), writes=[vaug[i]])

    def load_w(wd, l, k0, nk, c0, ncol, slot=None, soff=0):
        s = slot if slot is not None else nws()
        src = wd.t.ap()[l, k0:k0 + nk * 128, c0:c0 + ncol].rearrange("(kt p) n -> p kt n", p=128)
        kb.dma("sp", s.t[:, 0:nk, soff:soff + ncol], src, reads=[wd], writes=[s], owner=s)
        return s

    def mm_group(ps_ap, psbuf, pairs, extra_reads, first=True, last=True):
        n = len(pairs)
        for i, (a, b) in enumerate(pairs):
            st = first and i == 0
            sp_ = last and i == n - 1
            kb.op("pe", lambda e, a=a, b=b, st=st, sp_=sp_: e.matmul(ps_ap, lhsT=a, rhs=b, start=st, stop=sp_),
                  reads=extra_reads, writes=[psbuf], inc=(i == n - 1))

    def rope_tile(ps, dst_f32_ap, dstbuf):
        xs, t1, t2 = ntmp(), ntmp(), ntmp()
        kb.op("act", lambda e: e.copy(out=xs.t[:], in_=ps.t[:, 0:T]), reads=[ps], writes=[xs])
        p2 = nps()
        kb.op("pe", lambda e: e.matmul(p2.t[:, 0:T], lhsT=rrot, rhs=xs.t[:], start=True, stop=True),
              reads=[cst, xs], writes=[p2])
        kb.op("dve", lambda e: e.tensor_tensor(out=t1.t[:], in0=xs.t[:], in1=ropec.t[:], op=ALU.mult),
              reads=[xs, ropec], writes=[t1])
        kb.op("dve", lambda e: e.tensor_tensor(out=t2.t[:], in0=p2.t[:, 0:T], in1=ropes.t[:], op=ALU.mult),
              reads=[p2, ropes], writes=[t2])
        kb.op("dve", lambda e: e.tensor_tensor(out=dst_f32_ap, in0=t1.t[:], in1=t2.t[:], op=ALU.add),
              reads=[t1, t2], writes=[dstbuf])

    def TT(eng, out_ap, outb, a_ap, ab, b_ap, bb, op):
        return kb.op(eng, lambda e: e.tensor_tensor(out=out_ap, in0=a_ap, in1=b_ap, op=op),
                     reads=[ab, bb], writes=[outb])

    def TS(eng, out_ap, outb, a_ap, ab, s1, s2, op0, op1=None, extra=()):
        if op1 is None:
            return kb.op(eng, lambda e: e.tensor_scalar(out=out_ap, in0=a_ap, scalar1=s1, scalar2=None, op0=op0),
                         reads=[ab] + list(extra), writes=[outb])
        return kb.op(eng, lambda e: e.tensor_scalar(out=out_ap, in0=a_ap, scalar1=s1, scalar2=s2, op0=op0, op1=op1),
                     reads=[ab] + list(extra), writes=[outb])

    def STT(out_ap, outb, a_ap, ab, sc, b_ap, bb, op0, op1, extra=()):
        return kb.op("dve", lambda e: e.scalar_tensor_tensor(out=out_ap, in0=a_ap, scalar=sc, in1=b_ap, op0=op0, op1=op1),
                     reads=[ab, bb] + list(extra), writes=[outb])

    def ACT(out_ap, outb, in_ap, inb, func, bias=None, scale=None, extra=()):
        kw = {}
        if bias is not None:
            kw["bias"] = bias
        if scale is not None:
            kw["scale"] = scale
        return kb.op("act", lambda e: e.activation(out=out_ap, in_=in_ap, func=func, **kw),
                     reads=[inb] + list(extra), writes=[outb])

    def layer_norm(gname, bname):
        mps = nps()
        mm_group(mps.t[:, 0:T], mps, [(onesD, xf.t[:, m, :]) for m in range(8)], [cst, xf])
        for m in range(8):
            TT("dve", xf.t[:, m, :], xf, xf.t[:, m, :], xf, mps.t[:, 0:T], mps, ALU.subtract)
        vps = nps()
        for m in range(8):
            sq = ntmp()
            ACT(sq.t[:], sq, xf.t[:, m, :], xf, AF.Square)
            kb.op("pe", lambda e, m=m, sq=sq: e.matmul(vps.t[:, 0:T], lhsT=onesD, rhs=sq.t[:], start=(m == 0), stop=(m == 7)),
                  reads=[cst, sq], writes=[vps], inc=True)
        sd = ntmp()
        rs = ntmp()
        ACT(sd.t[:], sd, vps.t[:, 0:T], vps, AF.Sqrt, bias=epsc, extra=[small])
        kb.op("dve", lambda e: e.reciprocal(out=rs.t[:], in_=sd.t[:]), reads=[sd], writes=[rs])
        for m in range(8):
            TT("dve", xf.t[:, m, :], xf, xf.t[:, m, :], xf, rs.t[:], rs, ALU.mult)
            TS("dve", xf.t[:, m, :], xf, xf.t[:, m, :], xf, pvc(gname, m), pvc(bname, m), ALU.mult, ALU.add, extra=[pv])
        kb.op("act", lambda e: e.copy(out=xb.t[:], in_=xf.t[:]), reads=[xf], writes=[xb])

    kb.op("pool", lambda e: e.memset(small.t[:, 0:1], LN_EPS), writes=[small])
    epsc = small.t[:, 0:1]

    if SAMPLE:
        kb.dma("sp", idx.t[:], ptab_d.t.ap(), reads=[ptab_d], writes=[idx], owner=idx)
        idf = ntmp()
        kb.op("dve", lambda e: e.tensor_copy(out=idf.t[:], in_=idx.t[:]), reads=[idx], writes=[idf])
        TS("dve", idf.t[:], idf, idf.t[:], idf, 128.0, cst.t[:, 1088:1089], ALU.mult, ALU.add, extra=[cst])
        kb.op("dve", lambda e: e.tensor_copy(out=idx.t[:], in_=idf.t[:]), reads=[idf], writes=[idx])
        kb.dma("pool", cmaskS.t[:], cmaskS_d.t.ap(), reads=[cmaskS_d], writes=[cmaskS], owner=cmaskS)
        kb.op("pool", lambda e: e.memset(MBs.t[:], 0.0), writes=[MBs])
        small2 = kb.sb("small2", [128, 32], F32)
        pgb = kb.sb("pgb", [128, 512], F32)

    def sample_attention(l):
        NR = NPOOL * 128
        ck = cacheK.t.ap()
        cv = cacheV.t.ap()
        if l > 0:
            idf = ntmp()
            kb.op("dve", lambda e: e.tensor_copy(out=idf.t[:], in_=idx.t[:]), reads=[idx], writes=[idf])
            TS("dve", idf.t[:], idf, idf.t[:], idf, float(NR), None, ALU.add)
            kb.op("dve", lambda e: e.tensor_copy(out=idx.t[:], in_=idf.t[:]), reads=[idf], writes=[idx])
        Kpg = vtok[1]
        usf_flat = usf.t[:].rearrange("p a b -> p (a b)")
        Vpg = usf_flat[:, 0:512]
        Ksq = usf_flat[:, 512:1024]
        Oacc = [psb[6], psb[7]]
        kn2m = gsb.t[:, 0, :]
        kn2p = gsb.t[:, 1, :]
        kb.op("pool", lambda e: e.memset(yc.t[:, :, 128:T], 0.0), writes=[yc])
        for h in range(8):
            kb.op("pool", lambda e, h=h: e.memset(MB[h].t[0:32, :], 0.0), writes=[MB[h]])
        for hp in range(4):
            ksq = ntmp()
            ACT(ksq.t[:], ksq, kf.t[:, hp, :], kf, AF.Square)
            ACT(gS5f.t[:, hp, :], gS5f, qb.t[:, hp, :], qb, AF.Square)
            for hh in range(2):
                h = hp * 2 + hh
                b0 = hh * 64
                pk = nps()
                kb.op("pe", lambda e: e.matmul(pk.t[0:1, 0:128], lhsT=ones64[b0:b0 + 64, 0:1], rhs=ksq.t[b0:b0 + 64, 0:128],
                                               start=True, stop=True), reads=[cst, ksq], writes=[pk])
                kb.op("dve", lambda e: e.tensor_reduce(out=kmax2.t[0:1, h:h + 1], in_=pk.t[0:1, 0:128], axis=AX.X, op=ALU.max),
                      reads=[pk], writes=[kmax2])
        for o_ in Oacc:
            kb.op("dve", lambda e, o_=o_: e.memset(o_.t[0:65, :], 0.0), writes=[o_])
        for sq_ in range(16):
            qs_ = slice(sq_ * 8, sq_ * 8 + 8)
            kb.op("pool", lambda e: e.memset(kn2m, 0.0), writes=[gsb])
            for j in range(16):
                col = sq_ * 16 + j
                Kpg = vtok[1] if j % 2 == 0 else pgb
                kb.dma_ind(Kpg, Kpg.t[:], ck, idx.t[:, col:col + 1], reads=[idx, cacheK])
                tp = nps()
                for hp in range(4):
                    kb.op("pe", lambda e, hp=hp: e.transpose(out=tp.t[:, hp * 128:(hp + 1) * 128],
                                                             in_=Kpg.t[:, hp * 128:(hp + 1) * 128], identity=ident),
                          reads=[Kpg, cst], writes=[tp], inc=(hp == 3))
                ws_ = wsl[j // 8]
                kb.op("act", lambda e: e.copy(out=ws_.t[:, j % 8, :], in_=tp.t[:, :]), reads=[tp], writes=[ws_])
                half_ = small2.t[:, (j % 2) * 4:(j % 2) * 4 + 4]
                kb.op("dve", lambda e: e.tensor_reduce(out=half_, in_=tp.t[:, :].rearrange("p (h t) -> p h t", h=4),
                                                       axis=AX.X, op=ALU.add), reads=[tp], writes=[small2])
                if j % 2 == 1:
                    TT("dve", small2.t[:, 0:4], small2, small2.t[:, 0:4], small2, small2.t[:, 4:8], small2, ALU.add)
                    kb.op("act", lambda e: e.mul(out=KMs.t[:, :, j // 2], in_=small2.t[:, 0:4], mul=1.0 / 256.0),
                          reads=[small2], writes=[KMs])
                ACT(Ksq, usf, Kpg.t[:], Kpg, AF.Square)
                kb.op("dve", lambda e: e.tensor_reduce(out=kn2p, in_=Ksq.rearrange("p (h d) -> p h d", h=8), axis=AX.X, op=ALU.add),
                      reads=[usf], writes=[gsb])
                TT("dve", kn2m, gsb, kn2m, gsb, kn2p, gsb, ALU.max)
            Gsb = [nps(), nps()]
            for h in range(8):
                hp, b0 = h // 2, (h % 2) * 64
                kb.op("pe", lambda e: e.matmul(Gsb[h % 2].t[0:8, (h // 2) * 8:(h // 2 + 1) * 8], lhsT=qb.t[b0:b0 + 64, hp, qs_],
                                               rhs=KMs.t[b0:b0 + 64, hp, 0:8], start=True, stop=True),
                      reads=[qb, KMs], writes=[Gsb[h % 2]], inc=(h >= 6))
            for h in range(8):
                gsrc = Gsb[h % 2].t[0:8, (h // 2) * 8:(h // 2 + 1) * 8]
                kb.op("dve", lambda e: e.max(out=top8.t[0:8, h, :], in_=gsrc), reads=[Gsb[h % 2]], writes=[top8])
                TS("dve", selb[0].t[0:8, h, 0:8], selb[0], gsrc, Gsb[h % 2], top8.t[0:8, h, 2:3], -BIG,
                   ALU.is_lt, ALU.mult, extra=[top8])
            tpm = nps()
            for h in range(8):
                kb.op("pe", lambda e: e.transpose(out=tpm.t[0:8, h * 8:(h + 1) * 8], in_=selb[0].t[0:8, h, 0:8],
                                                  identity=ident[0:8, 0:8]), reads=[selb[0], cst], writes=[tpm], inc=(h == 7))
            kb.op("act", lambda e: e.copy(out=MBs.t[0:8, :], in_=tpm.t[0:8, 0:64]), reads=[tpm], writes=[MBs])
            tk_ = nps()
            kb.op("pe", lambda e: e.transpose(out=tk_.t[0:8, 0:128], in_=kn2m, identity=ident), reads=[gsb, cst], writes=[tk_])
            kb.op("dve", lambda e: e.tensor_reduce(out=small2.t[0:8, 8:9], in_=tk_.t[0:8, 0:128], axis=AX.X, op=ALU.max),
                  reads=[tk_], writes=[small2])
            tk2 = nps()
            kb.op("pe", lambda e: e.transpose(out=tk2.t[0:1, 0:8], in_=small2.t[0:8, 8:9], identity=ident[0:8, 0:8]),
                  reads=[small2, cst], writes=[tk2])
            TT("dve", small2.t[0:1, 16:24], small2, tk2.t[0:1, 0:8], tk2, kmax2.t[0:1, 0:8], kmax2, ALU.max)
            pqb = [nps(), nps()]
            for h in range(8):
                hp, b0 = h // 2, (h % 2) * 64
                kb.op("pe", lambda e: e.matmul(pqb[h % 2].t[0:1, (h // 2) * 8:(h // 2 + 1) * 8], lhsT=ones64[b0:b0 + 64, 0:1],
                                               rhs=gS5f.t[b0:b0 + 64, hp, qs_], start=True, stop=True),
                      reads=[cst, gS5f], writes=[pqb[h % 2]], inc=(h >= 6))
            mr = ntmp()
            for h in range(8):
                TS("dve", mr.t[0:1, h * 8:(h + 1) * 8], mr, pqb[h % 2].t[0:1, (h // 2) * 8:(h // 2 + 1) * 8], pqb[h % 2],
                   small2.t[0:1, 16 + h:17 + h], None, ALU.mult, extra=[small2])
            ACT(mr.t[0:1, 0:64], mr, mr.t[0:1, 0:64], mr, AF.Sqrt)
            TS("dve", MBs.t[32:33, :], MBs, mr.t[0:1, 0:64], mr, -1.02, -0.5, ALU.mult, ALU.add)
            for h in range(8):
                kb.op("act", lambda e: e.copy(out=MB[h].t[32:33, qs_], in_=MBs.t[32:33, h * 8:(h + 1) * 8]),
                      reads=[MBs], writes=[MB[h]])
            for j in range(16):
                col = sq_ * 16 + j
                n = j // 2
                Vb_ = usf if j % 2 == 0 else pgb
                Vpg_ = Vpg if j % 2 == 0 else pgb.t[:]
                kb.dma_ind(Vb_, Vpg_, cv, idx.t[:, col:col + 1], reads=[idx, cacheV])
                va = vaug[1]
                kb.op("dve", lambda e: e.tensor_copy(out=va.t[:, :, 0:64], in_=Vpg_.rearrange("p (h d) -> p h d", h=8)),
                      reads=[Vb_], writes=[va])
                Sb = [nps(), nps()]
                ws_ = wsl[j // 8]
                for h in range(8):
                    hp, b0 = h // 2, (h % 2) * 64
                    S = Sb[h % 2]
                    sc_ = slice((h // 2) * 8, (h // 2 + 1) * 8)
                    kb.op("pe", lambda e: e.matmul(S.t[:, sc_], lhsT=ws_.t[b0:b0 + 64, j % 8, hp * 128:(hp + 1) * 128],
                                                   rhs=qb.t[b0:b0 + 64, hp, qs_], start=True, stop=False),
                          reads=[ws_, qb], writes=[S], inc=False)
                    kb.op("pe", lambda e: e.matmul(S.t[:, sc_], lhsT=etab.t[:, n * 128:(n + 1) * 128],
                                                   rhs=MBs.t[:, h * 8:(h + 1) * 8], start=False, stop=True),
                          reads=[etab, MBs], writes=[S], inc=(h >= 6))
                P_ = Pt[kvctr[2] % 2]
                kvctr[2] += 1
                for par in range(2):
                    kb.op("act", lambda e, par=par: e.activation(
                        out=P_.t[:, 0:64].rearrange("p (a b q) -> p a b q", b=2, q=8)[:, :, par, :],
                        in_=Sb[par].t[:, 0:32].rearrange("p (a q) -> p a q", q=8), func=AF.Exp),
                        reads=[Sb[par]], writes=[P_])
                for h in range(8):
                    o_ = Oacc[h // 4]
                    c0_ = (h % 4) * 128 + sq_ * 8
                    kb.op("pe", lambda e: e.matmul(o_.t[0:65, c0_:c0_ + 8], lhsT=va.t[:, h, :], rhs=P_.t[:, h * 8:(h + 1) * 8],
                                                   start=False, stop=False, skip_group_check=True),
                          reads=[va, P_], writes=[o_], inc=(h == 7))
        for h in range(8):
            hp, b0 = h // 2, (h % 2) * 64
            S = nps()
            kb.op("pe", lambda e: e.matmul(S.t[:, 0:128], lhsT=kbf.t[b0:b0 + 64, hp, 0:128], rhs=qb.t[b0:b0 + 64, hp, 0:128],
                                           start=True, stop=False), reads=[kbf, qb], writes=[S], inc=False)
            kb.op("pe", lambda e: e.matmul(S.t[:, 0:128], lhsT=etab.t[:, 0:128], rhs=MB[h].t[:, 0:128],
                                           start=False, stop=False), reads=[etab, MB[h]], writes=[S], inc=False)
            kb.op("pe", lambda e: e.matmul(S.t[:, 0:128], lhsT=identb.t[:], rhs=cmaskS.t[:], start=False, stop=True),
                  reads=[identb, cmaskS], writes=[S], inc=True)
            P_ = Pt[kvctr[2] % 2]
            kvctr[2] += 1
            ACT(P_.t[:, 0:128], P_, S.t[:, 0:128], S, AF.Exp)
            o_ = Oacc[h // 4]
            c0_ = (h % 4) * 128
            kb.op("pe", lambda e: e.matmul(o_.t[0:65, c0_:c0_ + 128], lhsT=vaug[0].t[:, h, :], rhs=P_.t[:, 0:128],
                                           start=False, stop=True, skip_group_check=True),
                  reads=[vaug[0], P_], writes=[o_], inc=True)
        for h in range(8):
            hp, b0 = h // 2, (h % 2) * 64
            o_ = Oacc[h // 4]
            c0_ = (h % 4) * 128
            kb.op("act", lambda e: e.copy(out=Osb.t[:, 0:128], in_=o_.t[0:65, c0_:c0_ + 128]), reads=[o_], writes=[Osb])
            rden = ntmp()
            kb.op("dve", lambda e: e.reciprocal(out=rden.t[64:65, 0:128], in_=Osb.t[64:65, 0:128]), reads=[Osb], writes=[rden])
            bc = nps()
            kb.op("pe", lambda e: e.matmul(bc.t[0:64, 0:128], lhsT=ones64[64:65, 0:64], rhs=rden.t[64:65, 0:128],
                                           start=True, stop=True), reads=[cst, rden], writes=[bc])
            TT("dve", yc.t[b0:b0 + 64, hp, 0:128], yc, Osb.t[0:64, 0:128], Osb, bc.t[0:64, 0:128], bc, ALU.mult)

    for l in range(NL):
        kb.dma("sp", pv.t[:], pv_d.t.ap()[l], reads=[pv_d], writes=[pv], owner=pv)
        if STAGE >= 2:
            kb.dma("sp", ps5.t[:], ps5_d.t.ap()[l], reads=[ps5_d], writes=[ps5], owner=ps5)
            for (dst, src, n) in [(rgwab, rgwa, 8), (rgwxb, rgwx, 8), (bretb, bret, 16), (bimtb, bimt, 16),
                                  (cretb, cret, 16), (cimtb, cimt, 16)]:
                kb.dma("pool", dst.t[:], src.t.ap()[l].rearrange("j p n -> p j n"), reads=[src], writes=[dst], owner=dst)
            kb.dma("pool", wglub.t[:], w_glu.t.ap()[l].rearrange("(kt p) n -> p kt n", p=128),
                   reads=[w_glu], writes=[wglub], owner=wglub)
            ACT(clam.t[:], clam, pv.t[:, PV["lam"]:PV["lam"] + 8], pv, AF.Exp, scale=-1.0)
            ACT(clam.t[:], clam, clam.t[:], clam, AF.Ln, bias=1.0)
            kb.op("act", lambda e: e.mul(out=clam.t[:], in_=clam.t[:], mul=-8.0), reads=[clam], writes=[clam])
            kb.op("pool", lambda e: e.memset(carryA.t[:], 0.0), writes=[carryA])
            kb.op("pool", lambda e: e.memset(cS5.t[:], 0.0), writes=[cS5])
            kb.op("pool", lambda e: e.memset(xr.t[:, :, 0:3], 0.0), writes=[xr])
            kb.op("pool", lambda e: e.memset(KM.t[:], 0.0), writes=[KM])
            kb.op("pool", lambda e: e.memset(kmax2.t[:], 0.0), writes=[kmax2])
            for i in range(2):
                kb.op("pool", lambda e, i=i: e.memset(selb[i].t[:], 0.0), writes=[selb[i]])
            for h in range(8):
                kb.op("pool", lambda e, h=h: e.memset(MB[h].t[:], 0.0), writes=[MB[h]])
            lre = ps5.t[:, 0:16]
            lim = ps5.t[:, 16:32]
            lst = ps5.t[:, 32:48]
            dtv = s5p.t[:, 0, :]
            mag = s5p.t[:, 1, :]
            th = s5p.t[:, 2, :]
            zr = s5p.t[:, 3, :]
            zi = s5p.t[:, 4, :]
            w1 = s5p.t[:, 5, :]
            w2 = s5p.t[:, 6, :]
            w3 = s5p.t[:, 7, :]
            ACT(dtv, s5p, lst, ps5, AF.Exp)
            TT("dve", w1, s5p, lre, ps5, dtv, s5p, ALU.mult)
            ACT(mag, s5p, w1, s5p, AF.Exp)
            TT("dve", th, s5p, lim, ps5, dtv, s5p, ALU.mult)
            TWO_PI = 2.0 * np.pi
            for quarter in range(4):
                ang = vtok[0]
                wk = vtok[1]
                for jj in range(4):
                    j = quarter * 4 + jj
                    TS("dve", ang.t[:, jj * 128:(jj + 1) * 128], ang, tau, cst, s5p.t[:, 2, j:j + 1], None, ALU.mult,
                       extra=[s5p])
                for (dstT, shift) in [(ST, 0.0), (CT, np.pi / 2)]:
                    TS("dve", wk.t[:], wk, ang.t[:], ang, float(shift), None, ALU.add)
                    ki = ntmp()
                    kiv = ki.t[:].bitcast(I32)
                    for half in range(2):
                        sl = slice(half * 256, (half + 1) * 256)
                        kb.op("dve", lambda e, sl=sl: e.tensor_scalar(out=kiv, in0=wk.t[:, sl], scalar1=1.0 / TWO_PI,
                                                                     scalar2=None, op0=ALU.mult),
                              reads=[wk], writes=[ki])
                        kf_ = ntmp()
                        kb.op("dve", lambda e, kf_=kf_: e.tensor_copy(out=kf_.t[:], in_=kiv), reads=[ki], writes=[kf_])
                        STT(wk.t[:, sl], wk, kf_.t[:], kf_, -TWO_PI, wk.t[:, sl], wk, ALU.mult, ALU.add)
                        m1 = ntmp()
                        TS("dve", m1.t[:], m1, wk.t[:, sl], wk, float(np.pi), -TWO_PI, ALU.is_gt, ALU.mult)
                        TT("dve", wk.t[:, sl], wk, wk.t[:, sl], wk, m1.t[:], m1, ALU.add)
                        m2 = ntmp()
                        TS("dve", m2.t[:], m2, wk.t[:, sl], wk, float(-np.pi), TWO_PI, ALU.is_lt, ALU.mult)
                        TT("dve", wk.t[:, sl], wk, wk.t[:, sl], wk, m2.t[:], m2, ALU.add)
                    ACT(dstT.t[:, quarter * 4:(quarter + 1) * 4, :].rearrange("p j n -> p (j n)"), dstT, wk.t[:], wk, AF.Sin)
            abr, abi, den = w1, w2, w3
            TT("dve", abr, s5p, mag, s5p, CT.t[:, :, 0], CT, ALU.mult)
            TT("dve", abi, s5p, mag, s5p, ST.t[:, :, 0], ST, ALU.mult)
            TS("dve", abr, s5p, abr, s5p, -1.0, None, ALU.add)
            sA = small.t[:, 1:2]
            t_a = ntmp()
            t_b = ntmp()
            ta, tb = t_a.t[:, 0:16], t_b.t[:, 0:16]
            TT("dve", den, s5p, lre, ps5, lre, ps5, ALU.mult)
            TT("dve", ta, t_a, lim, ps5, lim, ps5, ALU.mult)
            TT("dve", den, s5p, den, s5p, ta, t_a, ALU.add)
            kb.op("dve", lambda e: e.reciprocal(out=den, in_=den), reads=[s5p], writes=[s5p])
            TT("dve", ta, t_a, abr, s5p, lre, ps5, ALU.mult)
            TT("dve", tb, t_b, abi, s5p, lim, ps5, ALU.mult)
            TT("dve", ta, t_a, ta, t_a, tb, t_b, ALU.add)
            TT("dve", zr, s5p, ta, t_a, den, s5p, ALU.mult)
            TT("dve", ta, t_a, abi, s5p, lre, ps5, ALU.mult)
            TT("dve", tb, t_b, abr, s5p, lim, ps5, ALU.mult)
            TT("dve", ta, t_a, ta, t_a, tb, t_b, ALU.subtract)
            TT("dve", zi, s5p, ta, t_a, den, s5p, ALU.mult)
            for j in range(16):
                zrj = s5p.t[:, 3, j:j + 1]
                zij = s5p.t[:, 4, j:j + 1]
                TS("dve", MRE.t[:, j, :], MRE, CT.t[:, j, :], CT, zrj, None, ALU.mult, extra=[s5p])
                STT(MRE.t[:, j, :], MRE, ST.t[:, j, :], ST, zij, MRE.t[:, j, :], MRE, ALU.mult, ALU.add, extra=[s5p])
                TS("dve", MIM.t[:, j, :], MIM, ST.t[:, j, :], ST, zrj, None, ALU.mult, extra=[s5p])
                STT(MIM.t[:, j, :], MIM, CT.t[:, j, :], CT, zij, MIM.t[:, j, :], MIM, ALU.mult, ALU.subtract, extra=[s5p])

        xsrc = xT if l == 0 else x1T
        chunks = [("P", c) for c in range(NCH)] + ([("S", 0)] if SAMPLE else [])
        for (mode, c) in chunks:
            SM = (mode == "S")
            t0 = c * T
            if SM:
                if NCH > 0:
                    kb.dma("pool", conv_o.t.ap()[l], xr.t[:, :, T:T + 3], reads=[xr], writes=[conv_o], owner=xr)
                    kb.dma("pool", rglru_o.t.ap()[l], carryA.t[:], reads=[carryA], writes=[rglru_o], owner=carryA)
                    kb.dma("pool", s5_o.t.ap()[l], cS5.t[:], reads=[cS5], writes=[s5_o], owner=cS5)
                xs_src = xsT if l == 0 else x1sT
                kb.op("pool", lambda e: e.memset(xf.t[:, :, 128:T], 0.0), writes=[xf])
                kb.dma("sp", xf.t[:, :, 0:128], xs_src.t.ap().rearrange("(m p) t -> p m t", p=128),
                       reads=[xs_src], writes=[xf], owner=xf)
                kb.op("act", lambda e: e.copy(out=xb.t[:], in_=xf.t[:]), reads=[xf], writes=[xb])
                kb.dma("sp", ropec.t[:], ropecS_d.t.ap(), reads=[ropecS_d], writes=[ropec], owner=ropec)
                kb.dma("sp", ropes.t[:], ropesS_d.t.ap(), reads=[ropesS_d], writes=[ropes], owner=ropes)
                kb.dma("sp", xe_v[:, :, :, 0:3], convS_d.t.ap()[l], reads=[convS_d], writes=[hff], owner=hff)
                kb.dma("sp", h0s_v, rgS_d.t.ap()[l], reads=[rgS_d], writes=[hff], owner=hff)
                kb.dma("sp", s5s0_v, s5S_d.t.ap()[l], reads=[s5S_d], writes=[hff], owner=hff)
            else:
                kb.dma("sp", xf.t[:], xsrc.t.ap()[:, t0:t0 + T].rearrange("(m p) t -> p m t", p=128),
                       reads=[xsrc], writes=[xf], owner=xf)
                kb.op("act", lambda e: e.copy(out=xb.t[:], in_=xf.t[:]), reads=[xf], writes=[xb])
                kb.dma("sp", ropec.t[:], ropec_d.t.ap()[:, t0:t0 + T], reads=[ropec_d], writes=[ropec], owner=ropec)
                kb.dma("sp", ropes.t[:], ropes_d.t.ap()[:, t0:t0 + T], reads=[ropes_d], writes=[ropes], owner=ropes)
                if c > 0:
                    kb.op("pool", lambda e: e.tensor_copy(out=xr.t[:, :, 0:3], in_=xr.t[:, :, T:T + 3]), reads=[xr], writes=[xr])
            for s in cfg.get("strips", range(6)):
                ws = load_w(wb_in, l, 0, 8, s * 512, 512)
                if s == 5:
                    for ts in range(1 if SM else 2):
                        p = nps()
                        mm_group(p.t[:, :], p,
                                 [(xb.t[:, kt, ts * 128:(ts + 1) * 128], ws.t[:, kt, :]) for kt in range(8)],
                                 [xb, ws])
                        vt = vtok[ts]
                        va = vaug[ts]
                        kb.op("act", lambda e, vt=vt, p=p: e.copy(out=vt.t[:], in_=p.t[:]), reads=[p], writes=[vt])
                        kb.op("dve", lambda e, va=va, p=p: e.tensor_copy(
                            out=va.t[:, :, 0:64], in_=p.t[:].rearrange("p (h d) -> p h d", h=8)),
                            reads=[p], writes=[va])
                        tk0 = t0 + ts * 128
                        if SM:
                            kb.dma("pool", newvs.t.ap()[l], vt.t[:], reads=[vt], writes=[newvs], owner=vt)
                        else:
                            kb.dma("pool", newv.t.ap()[l, tk0:tk0 + 128, :], vt.t[:], reads=[vt], writes=[newv], owner=vt)
                            kb.dma("pool", Vs[l].t.ap()[:, :, tk0 // 128, :].rearrange("h p e -> p h e"), va.t[:],
                                   reads=[va], writes=[Vs[l]], owner=va)
                    continue
                for mi in range(4):
                    p = nps()
                    mm_group(p.t[:, 0:T], p,
                             [(ws.t[:, kt, mi * 128:(mi + 1) * 128], xb.t[:, kt, :]) for kt in range(8)], [xb, ws])
                    if s < 2:
                        m = s * 4 + mi
                        if SM:
                            kb.op("act", lambda e, m=m, p=p: e.copy(out=xe_v[:, m, :, 3:11],
                                                                   in_=p.t[:, 0:128].rearrange("p (s t) -> p s t", s=16)),
                                  reads=[p], writes=[hff])
                        else:
                            kb.op("act", lambda e, m=m, p=p: e.copy(out=xr.t[:, m, 3:3 + T], in_=p.t[:, 0:T]),
                                  reads=[p], writes=[xr])
                    elif s == 2:
                        kb.op("act", lambda e, mi=mi, p=p: e.copy(out=usf.t[:, mi, :], in_=p.t[:, 0:T]),
                              reads=[p], writes=[usf])
                        kb.op("act", lambda e, mi=mi: e.copy(out=usb.t[:, mi, :], in_=usf.t[:, mi, :]),
                              reads=[usf], writes=[usb])
                    elif s == 3:
                        tq = ntmp()
                        rope_tile(p, tq.t[:], tq)
                        kb.op("act", lambda e, mi=mi, tq=tq: e.mul(out=qb.t[:, mi, :], in_=tq.t[:], mul=0.125),
                              reads=[tq], writes=[qb])
                    elif s == 4:
                        rope_tile(p, kf.t[:, mi, :], kf)
                        kb.op("act", lambda e, mi=mi: e.copy(out=kbf.t[:, mi, :], in_=kf.t[:, mi, :]),
                              reads=[kf], writes=[kbf])
            if SM:
                kb.dma("pool", newkTs.t.ap()[l].rearrange("(m p) t -> p m t", p=128), kf.t[:, :, 0:128],
                       reads=[kf], writes=[newkTs], owner=kf)
            elif cfg.get("kout", True):
                kb.dma("pool", newkT.t.ap()[l, :, t0:t0 + T].rearrange("(m p) t -> p m t", p=128), kf.t[:],
                       reads=[kf], writes=[newkT], owner=kf)
                kb.dma("pool", Ks[l].t.ap()[:, t0:t0 + T].rearrange("(m p) t -> p m t", p=128), kbf.t[:],
                       reads=[kbf], writes=[Ks[l]], owner=kbf)
            if STAGE < 2:
                continue
            def rg_gen(m):
                    acc = ntmp()
                    if SM:
                        kb.op("pool", lambda e: e.memset(acc.t[:, 128:T], 0.0), writes=[acc])
                        accv = acc.t[:, 0:128].rearrange("p (s t) -> p s t", s=16)
                        TS("dve", accv, acc, xe_v[:, m, :, 0:8], hff, pvc("cw0", m), pvc("cb", m), ALU.mult, ALU.add, extra=[pv])
                        for j in range(1, 4):
                            STT(accv, acc, xe_v[:, m, :, j:j + 8], hff, pvc("cw%d" % j, m), accv, acc, ALU.mult, ALU.add,
                                extra=[pv])
                    else:
                        TS("dve", acc.t[:], acc, xr.t[:, m, 0:T], xr, pvc("cw0", m), pvc("cb", m), ALU.mult, ALU.add, extra=[pv])
                        for j in range(1, 4):
                            STT(acc.t[:], acc, xr.t[:, m, j:j + T], xr, pvc("cw%d" % j, m), acc.t[:], acc, ALU.mult, ALU.add,
                                extra=[pv])
                    xcb = ntmp()
                    xcbv = xcb.t[:].bitcast(BF16)[:, 0:T]
                    yield
                    kb.op("act", lambda e: e.copy(out=xcbv, in_=acc.t[:]), reads=[acc], writes=[xcb])
                    pa = nps()
                    yield
                    kb.op("pe", lambda e: e.matmul(pa.t[:, 0:T], lhsT=rgwab.t[:, m, :], rhs=xcbv, start=True, stop=True),
                          reads=[rgwab, xcb], writes=[pa])
                    px = nps()
                    yield
                    kb.op("pe", lambda e: e.matmul(px.t[:, 0:T], lhsT=rgwxb.t[:, m, :], rhs=xcbv, start=True, stop=True),
                          reads=[rgwxb, xcb], writes=[px])
                    r_ = ntmp()
                    i_ = ntmp()
                    yield
                    ACT(r_.t[:], r_, pa.t[:, 0:T], pa, AF.Sigmoid, bias=pvc("ba", m), extra=[pv])
                    yield
                    ACT(i_.t[:], i_, px.t[:, 0:T], px, AF.Sigmoid, bias=pvc("bx", m), extra=[pv])
                    a_ = ntmp()
                    yield
                    ACT(a_.t[:], a_, r_.t[:], r_, AF.Exp, scale=clam.t[:, m:m + 1], extra=[clam])
                    a2 = ntmp()
                    yield
                    TT("dve", a2.t[:], a2, a_.t[:], a_, a_.t[:], a_, ALU.mult)
                    yield
                    ACT(a2.t[:], a2, a2.t[:], a2, AF.Sqrt, bias=1.0, scale=-1.0)
                    yield
                    TT("dve", i_.t[:], i_, i_.t[:], i_, a2.t[:], a2, ALU.mult)
                    yield
                    TT("dve", i_.t[:], i_, i_.t[:], i_, acc.t[:], acc, ALU.mult)
                    h_ = ntmp()
                    if SM:
                        kb.op("pool", lambda e: e.memset(h_.t[:, 128:T], 0.0), writes=[h_])
                        for sq_ in range(16):
                            cs_ = slice(sq_ * 8, sq_ * 8 + 8)
                            kb.op("dve", lambda e: e.tensor_tensor_scan(out=h_.t[:, cs_], data0=a_.t[:, cs_], data1=i_.t[:, cs_],
                                                                        initial=h0s_v[:, m, sq_:sq_ + 1], op0=ALU.mult, op1=ALU.add),
                                  reads=[a_, i_, hff], writes=[h_])
                        kb.op("act", lambda e: e.copy(out=rgso_v[:, m, :], in_=h_.t[:, 0:128].rearrange("p (s t) -> p s t", s=16)[:, :, 7]),
                              reads=[h_], writes=[hff])
                    else:
                        kb.op("dve", lambda e: e.tensor_tensor_scan(out=h_.t[:], data0=a_.t[:], data1=i_.t[:],
                                                                    initial=carryA.t[:, m:m + 1], op0=ALU.mult, op1=ALU.add),
                              reads=[a_, i_, carryA], writes=[h_])
                        kb.op("act", lambda e: e.copy(out=carryA.t[:, m:m + 1], in_=h_.t[:, T - 1:T]), reads=[h_], writes=[carryA])
                    yield
                    kb.op("act", lambda e: e.copy(out=ya.t[:, m, :], in_=h_.t[:]), reads=[h_], writes=[ya])

            for grp in range(4):
                alive = [rg_gen(grp * 2 + i_) for i_ in range(2)]
                while alive:
                    for g_ in list(alive):
                        try:
                            next(g_)
                        except StopIteration:
                            alive.remove(g_)
            if STAGE < 3:
                continue
            yps = psb[6]
            for mo in range(4):
                def pair_gen(mo, jj, hreb_, himb_):
                        j = mo * 4 + jj
                        pre = nps()
                        pim = nps()
                        kb.op("pe", lambda e: e.matmul(pre.t[:, 0:T], lhsT=bretb.t[:, j, :], rhs=usb.t[:, mo, :], start=True, stop=True),
                              reads=[bretb, usb], writes=[pre])
                        kb.op("pe", lambda e: e.matmul(pim.t[:, 0:T], lhsT=bimtb.t[:, j, :], rhs=usb.t[:, mo, :], start=True, stop=True),
                              reads=[bimtb, usb], writes=[pim])
                        if SM and jj < 2 and mo == 0:
                            kb.op("pool", lambda e: e.memset(hreb_.t[:, 128:T], 0.0), writes=[hreb_])
                            kb.op("pool", lambda e: e.memset(himb_.t[:, 128:T], 0.0), writes=[himb_])
                        for sub in range(1 if SM else 2):
                            cs = slice(sub * 128, (sub + 1) * 128)
                            t1, t2, t3, t4 = ntmp(), ntmp(), ntmp(), ntmp()
                            H = 128
                            if SM:
                                v3 = lambda ap: ap.rearrange("p (s t) -> p s t", s=16)
                                tb = lambda X: X.t[:, j, 0:8].unsqueeze(1).broadcast_to([128, 16, 8])
                            else:
                                v3 = lambda ap: ap
                                tb = lambda X: X.t[:, j, :]
                            lo = lambda b_: v3(b_.t[:, 0:H])
                            hi = lambda b_: v3(b_.t[:, H:2 * H])
                            TT("dve", lo(t1), t1, v3(pre.t[:, cs]), pre, tb(MRE), MRE, ALU.mult)
                            yield
                            TT("dve", lo(t2), t2, v3(pim.t[:, cs]), pim, tb(MIM), MIM, ALU.mult)
                            yield
                            TT("dve", lo(t1), t1, lo(t1), t1, lo(t2), t2, ALU.subtract)
                            TT("dve", lo(t3), t3, v3(pim.t[:, cs]), pim, tb(MRE), MRE, ALU.mult)
                            yield
                            TT("dve", lo(t4), t4, v3(pre.t[:, cs]), pre, tb(MIM), MIM, ALU.mult)
                            yield
                            TT("dve", lo(t3), t3, lo(t3), t3, lo(t4), t4, ALU.add)
                            if SM:
                                for sq_ in range(16):
                                    c0_ = slice(sq_ * 8, sq_ * 8 + 8)
                                    c1_ = slice(H + sq_ * 8, H + sq_ * 8 + 8)
                                    kb.op("dve", lambda e: e.tensor_tensor_scan(out=t1.t[:, c1_], data0=s5p.t[:, 1, j:j + 1].broadcast_to([128, 8]), data1=t1.t[:, c0_],
                                                                                initial=s5s0_v[:, j, sq_, 0:1], op0=ALU.mult, op1=ALU.add),
                                          reads=[s5p, t1, hff], writes=[t1])
                                    yield
                                    kb.op("dve", lambda e: e.tensor_tensor_scan(out=t3.t[:, c1_], data0=s5p.t[:, 1, j:j + 1].broadcast_to([128, 8]), data1=t3.t[:, c0_],
                                                                                initial=s5s0_v[:, j, sq_, 1:2], op0=ALU.mult, op1=ALU.add),
                                          reads=[s5p, t3, hff], writes=[t3])
                                    yield
                            else:
                                kb.op("dve", lambda e: e.tensor_tensor_scan(out=t1.t[:, H:2 * H], data0=s5p.t[:, 1, j:j + 1].broadcast_to([128, 128]), data1=t1.t[:, 0:H],
                                                                            initial=cS5.t[:, j, 0:1], op0=ALU.mult, op1=ALU.add),
                                      reads=[s5p, t1, cS5], writes=[t1])
                                yield
                                kb.op("dve", lambda e: e.tensor_tensor_scan(out=t3.t[:, H:2 * H], data0=s5p.t[:, 1, j:j + 1].broadcast_to([128, 128]), data1=t3.t[:, 0:H],
                                                                            initial=cS5.t[:, j, 1:2], op0=ALU.mult, op1=ALU.add),
                                      reads=[s5p, t3, cS5], writes=[t3])
                                yield
                            TT("dve", lo(t2), t2, hi(t1), t1, tb(CT), CT, ALU.mult)
                            yield
                            TT("dve", hi(t2), t2, hi(t3), t3, tb(ST), ST, ALU.mult)
                            yield
                            TT("dve", lo(t2), t2, lo(t2), t2, hi(t2), t2, ALU.subtract)
                            TT("dve", lo(t4), t4, hi(t3), t3, tb(CT), CT, ALU.mult)
                            yield
                            TT("dve", hi(t4), t4, hi(t1), t1, tb(ST), ST, ALU.mult)
                            yield
                            TT("dve", lo(t4), t4, lo(t4), t4, hi(t4), t4, ALU.add)
                            if SM:
                                kb.op("act", lambda e: e.copy(out=s5so_v[:, j, :, 0], in_=lo(t2)[:, :, 7]), reads=[t2], writes=[hff])
                                kb.op("act", lambda e: e.copy(out=s5so_v[:, j, :, 1], in_=lo(t4)[:, :, 7]), reads=[t4], writes=[hff])
                            else:
                                kb.op("act", lambda e: e.copy(out=cS5.t[:, j, 0:1], in_=t2.t[:, H - 1:H]), reads=[t2], writes=[cS5])
                                kb.op("act", lambda e: e.copy(out=cS5.t[:, j, 1:2], in_=t4.t[:, H - 1:H]), reads=[t4], writes=[cS5])
                            kb.op("act", lambda e: e.copy(out=hreb_.t[:, cs], in_=t2.t[:, 0:H]), reads=[t2], writes=[hreb_])
                            kb.op("act", lambda e: e.mul(out=himb_.t[:, cs], in_=t4.t[:, 0:H], mul=-1.0), reads=[t4], writes=[himb_])
                        kb.op("pe", lambda e: e.matmul(yps.t[:, 0:T], lhsT=cretb.t[:, j, :], rhs=hreb_.t[:], start=(jj == 0), stop=False),
                              reads=[cretb, hreb_], writes=[yps], inc=True)
                        kb.op("pe", lambda e: e.matmul(yps.t[:, 0:T], lhsT=cimtb.t[:, j, :], rhs=himb_.t[:], start=False, stop=(jj == 3)),
                              reads=[cimtb, himb_], writes=[yps], inc=True)

                for grp in range(2):
                    gens = [pair_gen(mo, grp * 2 + i_, hrebs[i_], himbs[i_]) for i_ in range(2)]
                    alive = list(gens)
                    while alive:
                        for g_ in list(alive):
                            try:
                                next(g_)
                            except StopIteration:
                                alive.remove(g_)
                y_ = ntmp()
                STT(y_.t[:], y_, usf.t[:, mo, :], usf, pvc("s5d", mo), yps.t[:, 0:T], yps, ALU.mult, ALU.add, extra=[pv])
                y2 = ntmp()
                TT("dve", y2.t[:], y2, y_.t[:], y_, y_.t[:], y_, ALU.mult)
                TS("dve", y2.t[:], y2, y2.t[:], y2, 0.044715, 1.0, ALU.mult, ALU.add)
                TT("dve", y2.t[:], y2, y2.t[:], y2, y_.t[:], y_, ALU.mult)
                ACT(y2.t[:], y2, y2.t[:], y2, AF.Tanh, scale=0.7978845608028654)
                TS("dve", y2.t[:], y2, y2.t[:], y2, 1.0, 0.5, ALU.add, ALU.mult)
                TT("dve", gS5f.t[:, mo, :], gS5f, y2.t[:], y2, y_.t[:], y_, ALU.mult)
                kb.op("act", lambda e: e.copy(out=gS5b.t[:, mo, :], in_=gS5f.t[:, mo, :]), reads=[gS5f], writes=[gS5b])
            for mo in range(4):
                pz = nps()
                mm_group(pz.t[:, 0:T], pz, [(wglub.t[:, k, mo * 128:(mo + 1) * 128], gS5b.t[:, k, :]) for k in range(4)],
                         [wglub, gS5b])
                sg = ntmp()
                ACT(sg.t[:], sg, pz.t[:, 0:T], pz, AF.Sigmoid, bias=pvc("bglu", mo), extra=[pv])
                TT("dve", ys.t[:, mo, :], ys, gS5f.t[:, mo, :], gS5f, sg.t[:], sg, ALU.mult)
            if STAGE < 4:
                continue
            if not SM:
                ATT = cfg.get("att", "abdf")
                for hp in (range(4) if "a" in ATT else []):
                    rsum = small.t[:, 2 + hp:3 + hp]
                    kb.op("dve", lambda e: e.tensor_reduce(out=rsum, in_=kf.t[:, hp, :], axis=AX.X, op=ALU.add),
                          reads=[kf], writes=[small])
                    kb.op("act", lambda e: e.mul(out=KM.t[:, hp, c:c + 1], in_=rsum, mul=1.0 / 256.0), reads=[small], writes=[KM])
                for hp in (range(4) if "b" in ATT else []):
                    ksq = ntmp()
                    qsq = ntmp()
                    ACT(ksq.t[:], ksq, kf.t[:, hp, :], kf, AF.Square)
                    ACT(qsq.t[:], qsq, qb.t[:, hp, :], qb, AF.Square)
                    for hh in range(2):
                        h = hp * 2 + hh
                        b0 = hh * 64
                        pk = nps()
                        kb.op("pe", lambda e: e.matmul(pk.t[0:1, 0:T], lhsT=ones64[b0:b0 + 64, 0:1], rhs=ksq.t[b0:b0 + 64, :],
                                                       start=True, stop=True), reads=[cst, ksq], writes=[pk])
                        pq = nps()
                        kb.op("pe", lambda e: e.matmul(pq.t[0:1, 0:T], lhsT=ones64[b0:b0 + 64, 0:1], rhs=qsq.t[b0:b0 + 64, :],
                                                       start=True, stop=True), reads=[cst, qsq], writes=[pq])
                        km = small.t[0:1, 8:9]
                        kb.op("dve", lambda e: e.tensor_reduce(out=km, in_=pk.t[0:1, 0:T], axis=AX.X, op=ALU.max),
                              reads=[pk], writes=[small])
                        TT("dve", kmax2.t[0:1, h:h + 1], kmax2, kmax2.t[0:1, h:h + 1], kmax2, km, small, ALU.max)
                        mr = ntmp()
                        TS("dve", mr.t[0:1, :], mr, pq.t[0:1, 0:T], pq, kmax2.t[0:1, h:h + 1], None, ALU.mult, extra=[kmax2])
                        ACT(mr.t[0:1, :], mr, mr.t[0:1, :], mr, AF.Sqrt)
                        TS("dve", MB[h].t[32:33, :], MB[h], mr.t[0:1, :], mr, -1.02, -0.5, ALU.mult, ALU.add)
                if c >= 3:
                    for qs in range(2):
                        Gb = [nps(), nps()]
                        Gvs = [g_.t[:, 0:128].rearrange("p (h n) -> p h n", h=4) for g_ in Gb]
                        for h in range(8):
                            hp, b0 = h // 2, (h % 2) * 64
                            kb.op("pe", lambda e, h=h, hp=hp, b0=b0: e.matmul(
                                Gvs[h % 2][:, h // 2, 0:c], lhsT=qb.t[b0:b0 + 64, hp, qs * 128:(qs + 1) * 128],
                                rhs=KM.t[b0:b0 + 64, hp, 0:c], start=True, stop=True), reads=[qb, KM], writes=[Gb[h % 2]], inc=(h >= 6))
                        if c < 8:
                            kb.op("pool", lambda e: e.memset(gsb.t[:], -1e30), writes=[gsb])
                            for par in range(2):
                                kb.op("dve", lambda e, par=par: e.tensor_copy(
                                    out=gsb.t[:, :, 0:c].rearrange("p (a b) n -> p a b n", b=2)[:, :, par, :], in_=Gvs[par][:, :, 0:c]),
                                    reads=[Gb[par]], writes=[gsb])
                        for h in range(8):
                            gsrc = Gvs[h % 2][:, h // 2, 0:c]
                            src = gsb.t[:, h, :] if c < 8 else gsrc
                            srcb = gsb if c < 8 else Gb[h % 2]
                            kb.op("dve", lambda e, h=h, src=src: e.max(out=top8.t[:, h, :], in_=src), reads=[srcb], writes=[top8])
                            TS("dve", selb[qs].t[:, h, 0:c], selb[qs], gsrc, Gb[h % 2], top8.t[:, h, 2:3], -BIG,
                               ALU.is_lt, ALU.mult, extra=[top8])
                    for h in range(8):
                        tp = nps()
                        for qs in range(2):
                            kb.op("pe", lambda e, qs=qs: e.transpose(out=tp.t[0:32, qs * 128:(qs + 1) * 128], in_=selb[qs].t[:, h, :],
                                                                     identity=ident), reads=[selb[qs], cst], writes=[tp], inc=(qs == 1))
                        kb.op("act", lambda e, h=h, tp=tp: e.copy(out=MB[h].t[0:32, :], in_=tp.t[0:32, 0:T]), reads=[tp], writes=[MB[h]])
                nkt = 2 * c + 2
                ngrp = (nkt + 15) // 16
                for hp in (range(4) if "d" in ATT else []):
                    Ops = [psb[6], psb[7]]
                    for g in range(ngrp):
                        kt0 = g * 16
                        nk_g = min(16, nkt - kt0)
                        kbu = kbuf[kvctr[0] % 2]
                        kvctr[0] += 1
                        kb.dma("sp", kbu.t[:, 0:nk_g * 128], Ks[l].t.ap()[hp * 128:(hp + 1) * 128, kt0 * 128:(kt0 + nk_g) * 128],
                               reads=[Ks[l]], writes=[kbu], owner=kbu)
                        for hh in range(2):
                            h = hp * 2 + hh
                            b0 = hh * 64
                            vbu = vbuf[kvctr[1] % 2]
                            kvctr[1] += 1
                            kb.dma("sp", vbu.t[:, 0:nk_g, :], Vs[l].t.ap()[h, :, kt0:kt0 + nk_g, :], reads=[Vs[l]], writes=[vbu], owner=vbu)
                            pend = None
                            for kk in range(nk_g):
                                kt = kt0 + kk
                                n = kt // 2
                                own = kt >= 2 * c
                                S = nps()
                                kb.op("pe", lambda e: e.matmul(S.t[:, 0:T], lhsT=kbu.t[b0:b0 + 64, kk * 128:(kk + 1) * 128],
                                                               rhs=qb.t[b0:b0 + 64, hp, :], start=True, stop=False),
                                      reads=[kbu, qb], writes=[S], inc=False)
                                kb.op("pe", lambda e: e.matmul(S.t[:, 0:T], lhsT=etab.t[:, n * 128:(n + 1) * 128],
                                                               rhs=MB[h].t[:, :], start=False, stop=(not own)),
                                      reads=[etab, MB[h]], writes=[S], inc=(not own))
                                if own:
                                    kb.op("pe", lambda e: e.matmul(S.t[:, 0:T], lhsT=identb.t[:], rhs=causb.t[:, kt - 2 * c, :],
                                                                   start=False, stop=True),
                                          reads=[identb, causb], writes=[S], inc=True)
                                P_ = Pt[kvctr[2] % 2]
                                kvctr[2] += 1
                                ACT(P_.t[:], P_, S.t[:, 0:T], S, AF.Exp)

                                def emit_pv(P_=P_, kk=kk, kt=kt, vbu=vbu, hh=hh):
                                    kb.op("pe", lambda e: e.matmul(Ops[hh].t[0:65, 0:T], lhsT=vbu.t[:, kk, :], rhs=P_.t[:],
                                                                   start=(kt == 0), stop=(kt == nkt - 1)),
                                          reads=[vbu, P_], writes=[Ops[hh]], inc=True)
                                if pend is not None:
                                    pend()
                                pend = emit_pv
                            if pend is not None:
                                pend()
                    for hh in (range(2) if "f" in ATT else []):
                        b0 = hh * 64
                        kb.op("act", lambda e: e.copy(out=Osb.t[:], in_=Ops[hh].t[0:65, 0:T]), reads=[Ops[hh]], writes=[Osb])
                        rden = ntmp()
                        kb.op("dve", lambda e: e.reciprocal(out=rden.t[64:65, :], in_=Osb.t[64:65, :]), reads=[Osb], writes=[rden])
                        bc = nps()
                        kb.op("pe", lambda e: e.matmul(bc.t[0:64, 0:T], lhsT=ones64[64:65, 0:64], rhs=rden.t[64:65, :],
                                                       start=True, stop=True), reads=[cst, rden], writes=[bc])
                        TT("dve", yc.t[b0:b0 + 64, hp, :], yc, Osb.t[0:64, :], Osb, bc.t[0:64, 0:T], bc, ALU.mult)

            else:
                sample_attention(l)
                cvv = vtok[1].t[:, 0:384].rearrange("p (m s j) -> p m s j", m=8, s=16)
                kb.op("dve", lambda e: e.tensor_copy(out=cvv, in_=xe_v[:, :, :, 8:11]), reads=[hff], writes=[vtok[1]])
                kb.dma("pool", convs_o.t.ap()[l], cvv, reads=[vtok[1]], writes=[convs_o], owner=vtok[1])
                kb.dma("pool", rglrus_o.t.ap()[l], rgso_v, reads=[hff], writes=[rglrus_o], owner=hff)
                kb.dma("pool", s5s_o.t.ap()[l], s5so_v, reads=[hff], writes=[s5s_o], owner=hff)
            if STAGE < 5:
                continue
            branches = [(wb_bra, ya, 8), (wb_brs, ys, 4), (wb_brc, yc, 4)]
            for half in range(2):
                for i, (wbr, ybr, nk) in enumerate(branches):
                    wg = load_w(wb_in, l, 0, 8, 3072 + i * 1024 + half * 512, 512)
                    wr = load_w(wbr, l, 0, nk, half * 512, 512)
                    for mi in range(4):
                        pg = nps()
                        mm_group(pg.t[:, 0:T], pg, [(wg.t[:, kt, mi * 128:(mi + 1) * 128], xb.t[:, kt, :]) for kt in range(8)],
                                 [wg, xb])
                        sg = ntmp()
                        ACT(sg.t[:], sg, pg.t[:, 0:T], pg, AF.Sigmoid)
                        pbr = nps()
                        mm_group(pbr.t[:, 0:T], pbr, [(wr.t[:, kt, mi * 128:(mi + 1) * 128], ybr.t[:, kt, :]) for kt in range(nk)],
                                 [wr, ybr])
                        if i == 0:
                            TT("dve", gS5f.t[:, mi, :], gS5f, sg.t[:], sg, pbr.t[:, 0:T], pbr, ALU.mult)
                        else:
                            TT("dve", sg.t[:], sg, sg.t[:], sg, pbr.t[:, 0:T], pbr, ALU.mult)
                            TT("dve", gS5f.t[:, mi, :], gS5f, gS5f.t[:, mi, :], gS5f, sg.t[:], sg, ALU.add)
                for mi in range(4):
                    m = half * 4 + mi
                    kb.op("act", lambda e, m=m, mi=mi: e.copy(out=mg.t[:, m, :], in_=gS5f.t[:, mi, :]), reads=[gS5f], writes=[mg])
            for half in range(2):
                wo = load_w(wb_out, l, 0, 8, half * 512, 512)
                for mi in range(4):
                    m = half * 4 + mi
                    po = nps()
                    mm_group(po.t[:, 0:T], po, [(wo.t[:, kt, mi * 128:(mi + 1) * 128], mg.t[:, kt, :]) for kt in range(8)],
                             [wo, mg])
                    STT(xf.t[:, m, :], xf, xf.t[:, m, :], xf, ALPHA, po.t[:, 0:T], po, ALU.mult, ALU.add)
            layer_norm("g1", "b1")
            for s in range(11):
                wf = nws()
                load_w(wb_f1, l, 0, 8, s * 256, 256, slot=wf, soff=0)
                load_w(wb_f1, l, 0, 8, DFF + s * 256, 256, slot=wf, soff=256)
                for mi in range(2):
                    pg = nps()
                    mm_group(pg.t[:, 0:T], pg, [(wf.t[:, kt, mi * 128:(mi + 1) * 128], xb.t[:, kt, :]) for kt in range(8)], [wf, xb])
                    pu = nps()
                    mm_group(pu.t[:, 0:T], pu, [(wf.t[:, kt, 256 + mi * 128:256 + (mi + 1) * 128], xb.t[:, kt, :]) for kt in range(8)],
                             [wf, xb])
                    sg = ntmp()
                    ACT(sg.t[:], sg, pg.t[:, 0:T], pg, AF.Silu)
                    TT("dve", hff.t[:, 2 * s + mi, :], hff, sg.t[:], sg, pu.t[:, 0:T], pu, ALU.mult)
            for half in range(2):
                pacc = [nps() for _ in range(4)]
                for ks, nk in [(0, 8), (8, 8), (16, 6)]:
                    w2 = load_w(wb_f2, l, ks * 128, nk, half * 512, 512)
                    for mi in range(4):
                        mm_group(pacc[mi].t[:, 0:T], pacc[mi],
                                 [(w2.t[:, kt, mi * 128:(mi + 1) * 128], hff.t[:, ks + kt, :]) for kt in range(nk)],
                                 [w2, hff], first=(ks == 0), last=(ks == 16))
                for mi in range(4):
                    m = half * 4 + mi
                    STT(xf.t[:, m, :], xf, xf.t[:, m, :], xf, ALPHA, pacc[mi].t[:, 0:T], pacc[mi], ALU.mult, ALU.add)
            layer_norm("g2", "b2")
            if SM:
                dst = ysT if l == NL - 1 else x1sT
                kb.dma("pool", dst.t.ap().rearrange("(m p) t -> p m t", p=128), xf.t[:, :, 0:128],
                       reads=[xf], writes=[dst], owner=xf)
            else:
                dst = yT if l == NL - 1 else x1T
                kb.dma("pool", dst.t.ap()[:, t0:t0 + T].rearrange("(m p) t -> p m t", p=128), xf.t[:],
                       reads=[xf], writes=[dst], owner=xf)
        if STAGE >= 2 and not SAMPLE:
            kb.dma("pool", conv_o.t.ap()[l], xr.t[:, :, T:T + 3], reads=[xr], writes=[conv_o], owner=xr)
            kb.dma("pool", rglru_o.t.ap()[l], carryA.t[:], reads=[carryA], writes=[rglru_o], owner=carryA)
        if STAGE >= 3 and not SAMPLE:
            kb.dma("pool", s5_o.t.ap()[l], cS5.t[:], reads=[cS5], writes=[s5_o], owner=cS5)

    outs = [yT, newkT, newv, conv_o, rglru_o, s5_o]
    if SAMPLE:
        outs += [ysT, newkTs, newvs, convs_o, rglrus_o, s5s_o, x1sT]
    kb.wait_all("sp", outs)
    kb.wait_all("sp", Ks + Vs + [x1T])
    kb.declared = declared
    return nc, es, kb


def _consts():
    cst = np.zeros((128, NCST), np.float32)
    cst[:, 0:128] = np.eye(128, dtype=np.float32)
    R = np.zeros((128, 128), np.float32)
    for m in range(128):
        if (m % 64) < 32:
            R[m + 32, m] = -1.0
        else:
            R[m - 32, m] = 1.0
    cst[:, 128:256] = R
    kk = np.arange(128)[:, None]
    qi = np.arange(T)[None, :]
    for a in range(2):
        cst[:, 256 + a * T:256 + (a + 1) * T] = np.where(128 * a + kk <= qi, 0.0, -BIG)
    cst[:, 768:896] = np.arange(1, 129, dtype=np.float32)[None, :]
    cst[:, 896:960] = 1.0
    cst[:, 960:1088] = 1.0 / 1024.0
    cst[:, 1088] = np.arange(128, dtype=np.float32)
    etab = np.zeros((128, 32, 128), np.float32)
    for n in range(32):
        etab[n, n, :] = 1.0
    etab[32, :, :] = 1.0
    return cst, etab.reshape(128, 32 * 128)


def _rope_tables(SEQ=SEQ):
    half = 32
    inv = np.power(np.float32(10000.0), -np.arange(half, dtype=np.float32) * np.float32(2.0 / 64)).astype(np.float32)
    pos = np.arange(SEQ, dtype=np.float32)
    ang = (pos[None, :] * inv[:, None]).astype(np.float32)
    c = np.cos(ang).astype(np.float32)
    s = np.sin(ang).astype(np.float32)
    ropec = np.tile(c, (4, 1))
    ropes = np.tile(s, (4, 1))
    return np.ascontiguousarray(ropec), np.ascontiguousarray(ropes)


def _pm(v, ntile):
    L = v.shape[0]
    return np.ascontiguousarray(v.reshape(L, ntile, 128).transpose(0, 2, 1))


def _prep_shared(inp, SEQ=SEQ):
    f = np.float32
    sh = {}
    for k in ["w_in", "w_br_rnn", "w_br_ssm", "w_br_attn", "w_out", "w_ffn_in", "w_ffn_out", "s5_w_glu"]:
        sh[k] = np.ascontiguousarray(inp[k], dtype=f)
    for nm, src in [("rgwa", "rg_w_a"), ("rgwx", "rg_w_x")]:
        w = np.asarray(inp[src], f)
        o = np.zeros((DEPTH, 8, 128, 128), f)
        for j in range(8):
            o[:, j, 0:64, 0:64] = w[:, 2 * j]
            o[:, j, 64:128, 64:128] = w[:, 2 * j + 1]
        sh[nm] = o
    bre = np.asarray(inp["s5_b_re"], f)
    bim = np.asarray(inp["s5_b_im"], f)
    cre = np.asarray(inp["s5_c_re"], f)
    cim = np.asarray(inp["s5_c_im"], f)
    bret = np.zeros((DEPTH, 16, 128, 128), f)
    bimt = np.zeros((DEPTH, 16, 128, 128), f)
    cret = np.zeros((DEPTH, 16, 128, 128), f)
    cimt = np.zeros((DEPTH, 16, 128, 128), f)
    for j in range(16):
        for gl in range(2):
            g = 2 * j + gl
            r0 = 16 * (g % 8)
            bret[:, j, r0:r0 + 16, gl * 64:(gl + 1) * 64] = bre[:, g].transpose(0, 2, 1)
            bimt[:, j, r0:r0 + 16, gl * 64:(gl + 1) * 64] = bim[:, g].transpose(0, 2, 1)
            cret[:, j, gl * 64:(gl + 1) * 64, r0:r0 + 16] = cre[:, g].transpose(0, 2, 1)
            cimt[:, j, gl * 64:(gl + 1) * 64, r0:r0 + 16] = cim[:, g].transpose(0, 2, 1)
    sh["bret"], sh["bimt"], sh["cret"], sh["cimt"] = bret, bimt, cret, cimt
    pv = np.zeros((DEPTH, 128, NPV), f)
    cw = np.asarray(inp["conv_w"], f)
    for j in range(4):
        pv[:, :, PV["cw%d" % j]:PV["cw%d" % j] + 8] = _pm(cw[:, j], 8)
    for nm, src, nt in [("cb", "conv_b", 8), ("ba", "rg_b_a", 8), ("bx", "rg_b_x", 8), ("lam", "rg_lambda", 8),
                        ("s5d", "s5_d", 4), ("bglu", "s5_b_glu", 4), ("g1", "ln1_g", 8), ("b1", "ln1_b", 8),
                        ("g2", "ln2_g", 8), ("b2", "ln2_b", 8)]:
        pv[:, :, PV[nm]:PV[nm] + nt] = _pm(np.asarray(inp[src], f), nt)
    sh["pv"] = pv
    ps5 = np.zeros((DEPTH, 128, 48), f)
    lre = np.asarray(inp["s5_lambda_re"], f)
    lim = np.asarray(inp["s5_lambda_im"], f)
    lst = np.asarray(inp["s5_log_step"], f)
    for j in range(16):
        for gl in range(2):
            g = 2 * j + gl
            ps5[:, gl * 64:(gl + 1) * 64, j] = lre[:, g]
            ps5[:, gl * 64:(gl + 1) * 64, 16 + j] = lim[:, g]
            ps5[:, gl * 64:(gl + 1) * 64, 32 + j] = lst[:, g][:, None]
    sh["ps5"] = ps5
    sh["ropec"], sh["ropes"] = _rope_tables(SEQ)
    sh["cst"], sh["etab"] = _consts()
    return sh


_CFG = {}


def _prep_sample(inp, c, npool=2560, page_table=None):
    f = np.float32
    m = {}
    b0 = 16 * c
    xs = np.asarray(inp["x_sample"], f)[b0:b0 + 16]
    m["xsT"] = np.ascontiguousarray(xs.reshape(128, D).T)
    half = 32
    inv = np.power(np.float32(10000.0), -np.arange(half, dtype=np.float32) * np.float32(2.0 / 64)).astype(np.float32)
    pos = (2048 + np.arange(8)).astype(np.float32)
    ang = (pos[None, :] * inv[:, None]).astype(np.float32)
    cc = np.zeros((128, T), f)
    ss = np.zeros((128, T), f)
    cc[:, 0:128] = np.tile(np.tile(np.cos(ang).astype(f), (4, 1)), (1, 16))
    ss[:, 0:128] = np.tile(np.tile(np.sin(ang).astype(f), (4, 1)), (1, 16))
    m["ropecS"], m["ropesS"] = cc, ss
    sc = np.asarray(inp["state_conv"], f)[:, b0:b0 + 16]
    m["convS"] = np.ascontiguousarray(sc.reshape(DEPTH, 16, 3, 8, 128).transpose(0, 4, 3, 1, 2))
    sr = np.asarray(inp["state_rglru"], f)[:, b0:b0 + 16]
    m["rgS"] = np.ascontiguousarray(sr.reshape(DEPTH, 16, 8, 128).transpose(0, 3, 2, 1))
    s5 = np.stack([np.asarray(inp["state_s5_re"], f)[:, b0:b0 + 16], np.asarray(inp["state_s5_im"], f)[:, b0:b0 + 16]], -1)
    s5 = s5.reshape(DEPTH, 16, 16, 2, 64, 2).transpose(0, 3, 4, 2, 1, 5).reshape(DEPTH, 128, 16, 16, 2)
    m["s5S"] = np.ascontiguousarray(s5)
    pt = np.asarray(inp["page_table"] if page_table is None else page_table, np.int32)[b0:b0 + 16]
    m["ptab"] = np.ascontiguousarray(np.broadcast_to(pt.reshape(1, 256), (128, 256))).astype(np.int32)
    kk = np.arange(128)
    same = (kk[:, None] // 8) == (kk[None, :] // 8)
    caus = (kk[:, None] % 8) <= (kk[None, :] % 8)
    m["cmaskS"] = np.where(same & caus, 0.0, -BIG).astype(f)
    return m


def run(inp, cfg):
    nc, es, kbld = build(cfg)
    sq = cfg.get("seq", SEQ)
    sh = _prep_shared(inp, sq)
    xp = np.asarray(inp["x_prompt"], np.float32)[:, :sq]
    in_maps = []
    ncores = cfg.get("ncores", 8)
    if cfg.get("sample", True):
        npool = cfg.get("npool", 2560)
        ckf = np.asarray(inp["cache_k"], np.float32).reshape(DEPTH * npool * 128, 512)
        cvf = np.asarray(inp["cache_v"], np.float32).reshape(DEPTH * npool * 128, 512)
    for c in range(ncores):
        m = dict(sh)
        m["xT"] = np.ascontiguousarray(xp[c // 4].T)
        if cfg.get("sample", True):
            m.update(_prep_sample(inp, c))
            m["cache_k"] = ckf
            m["cache_v"] = cvf
        in_maps.append({k: v for k, v in m.items() if k in kbld.declared})
    with es:
        res = run_bass_kernel_spmd(nc, in_maps, core_ids=list(range(ncores)))
    return res.results


def kernel(**inputs):
    r = run(inputs, dict(_CFG))
    f = np.float32
    B, S, NS = 2, SEQ, 128
    yp = np.zeros((B, S, D), f)
    kp = np.zeros((DEPTH, B, S, 8, 64), f)
    vp = np.zeros((DEPTH, B, S, 8, 64), f)
    cp = np.zeros((DEPTH, B, 3, D), f)
    hp = np.zeros((DEPTH, B, D), f)
    s5rp = np.zeros((DEPTH, B, 32, 64), f)
    s5ip = np.zeros((DEPTH, B, 32, 64), f)
    for b in range(B):
        o = r[4 * b]
        yp[b] = o["yT"].T
        for l in range(DEPTH):
            kp[l, b] = o["newkT"][l].T.reshape(S, 8, 64)
            vp[l, b] = o["newv"][l].reshape(S, 8, 64)
            cp[l, b] = o["conv_o"][l].transpose(2, 1, 0).reshape(3, D)
            hp[l, b] = o["rglru_o"][l].T.reshape(D)
            s5 = o["s5_o"][l].reshape(2, 64, 16, 2).transpose(2, 0, 1, 3).reshape(32, 64, 2)
            s5rp[l, b] = s5[:, :, 0]
            s5ip[l, b] = s5[:, :, 1]
    ys = np.zeros((NS, 8, D), f)
    ks = np.zeros((DEPTH, NS, 8, 8, 64), f)
    vs = np.zeros((DEPTH, NS, 8, 8, 64), f)
    cs = np.zeros((DEPTH, NS, 3, D), f)
    hs = np.zeros((DEPTH, NS, D), f)
    s5rs = np.zeros((DEPTH, NS, 32, 64), f)
    s5is = np.zeros((DEPTH, NS, 32, 64), f)
    for c in range(8):
        o = r[c]
        b0 = 16 * c
        ys[b0:b0 + 16] = o["ysT"].T.reshape(16, 8, D)
        for l in range(DEPTH):
            ks[l, b0:b0 + 16] = o["newkTs"][l].T.reshape(16, 8, 8, 64)
            vs[l, b0:b0 + 16] = o["newvs"][l].reshape(16, 8, 8, 64)
            cs[l, b0:b0 + 16] = o["convs_o"][l].transpose(2, 3, 1, 0).reshape(16, 3, D)
            hs[l, b0:b0 + 16] = o["rglrus_o"][l].transpose(2, 1, 0).reshape(16, D)
            s5 = o["s5s_o"][l].reshape(2, 64, 16, 16, 2).transpose(3, 2, 0, 1, 4).reshape(16, 32, 64, 2)
            s5rs[l, b0:b0 + 16] = s5[..., 0]
            s5is[l, b0:b0 + 16] = s5[..., 1]
    return (yp, ys, kp, vp, cp, hp, s5rp, s5ip, ks, vs, cs, hs, s5rs, s5is)
```
